# Optimizing a Trainium2 kernel written in Bass

```python
import jax, jax.numpy as jnp
from jax import lax
import numpy as np

D_MODEL = 1024
BATCH = 8
SEQ = 8192
DEPTH = 4

D_MIX = 2 * D_MODEL
GLA_HEADS = 4
GLA_VWIDTH = D_MODEL
GLA_KWIDTH = D_MODEL // 2
GLA_DK = GLA_KWIDTH // GLA_HEADS
GLA_DV = GLA_VWIDTH // GLA_HEADS
GLA_GATE_RANK = 16
GLA_GATE_NORMALIZER = 16.0
GLA_CHUNK = 64
SC_WIDTH = D_MODEL // 2
SC_CONV = 3
CF_WIDTH = D_MODEL // 2
CF_CONV = 31

_SPLIT_SIZES = (
    GLA_KWIDTH,
    GLA_KWIDTH,
    GLA_VWIDTH,
    GLA_VWIDTH,
    GLA_GATE_RANK,
    SC_WIDTH,
    SC_WIDTH,
    SC_WIDTH,
    SC_WIDTH,
    CF_WIDTH,
    CF_WIDTH,
    CF_WIDTH,
)
D_IN = 2 * GLA_KWIDTH + 2 * GLA_VWIDTH + GLA_GATE_RANK + 4 * SC_WIDTH + 3 * CF_WIDTH

NORM_EPS = 1e-6
LN_EPS = 1e-5

kernel_name = "hybrid_gla_shortconv_conformer_trunk"


def split_points():
    pts = []
    acc = 0
    for s in _SPLIT_SIZES[:-1]:
        acc += s
        pts.append(acc)
    return tuple(pts)


def rms_norm(x, w, eps=NORM_EPS):
    xf = x.astype(jnp.float32)
    y = xf * lax.rsqrt(jnp.mean(xf * xf, axis=-1, keepdims=True) + eps)
    return (y * w.astype(jnp.float32)).astype(x.dtype)


def layer_norm(x, w, b, eps=LN_EPS):
    xf = x.astype(jnp.float32)
    mu = jnp.mean(xf, axis=-1, keepdims=True)
    xc = xf - mu
    var = jnp.mean(xc * xc, axis=-1, keepdims=True)
    y = xc * lax.rsqrt(var + eps) * w.astype(jnp.float32) + b.astype(jnp.float32)
    return y.astype(x.dtype)


def causal_depthwise_conv(u, w, b=None):
    k_width, ch = w.shape
    y = lax.conv_general_dilated(
        u, w[:, None, :].astype(u.dtype),
        window_strides=(1,),
        padding=[(k_width - 1, 0)],
        dimension_numbers=("NWC", "WIO", "NWC"),
        feature_group_count=ch)
    if b is not None:
        y = y + b.astype(y.dtype)
    return y


def gla_chunked(q, k, v, log_alpha):
    bsz, t_len, n_h, dk = q.shape
    dv = v.shape[-1]
    n_chunks = t_len // GLA_CHUNK

    def to_chunks(t):
        return t.reshape(bsz, n_chunks, GLA_CHUNK, n_h, t.shape[-1]).transpose(0, 3, 1, 2, 4).astype(jnp.float32)

    qc = to_chunks(q) * (dk ** -0.5)
    kc = to_chunks(k)
    vc = to_chunks(v)
    b = jnp.cumsum(to_chunks(log_alpha), axis=3)
    b_last = b[:, :, :, -1:, :]
    b_ref = b[:, :, :, GLA_CHUNK // 2:GLA_CHUNK // 2 + 1, :]

    q_intra = qc * jnp.exp(b - b_ref)
    k_intra = kc * jnp.exp(b_ref - b)
    scores = jnp.einsum("bhncd,bhnsd->bhncs", q_intra, k_intra)
    causal = jnp.tril(jnp.ones((GLA_CHUNK, GLA_CHUNK), dtype=bool))
    scores = jnp.where(causal, scores, 0.0)
    o_intra = jnp.einsum("bhncs,bhnse->bhnce", scores, vc)

    q_inter = qc * jnp.exp(b)
    k_inter = kc * jnp.exp(b_last - b)
    chunk_decay = jnp.exp(b_last[:, :, :, 0, :])

    def step(state, xs):
        qi, ki, vi, di = xs
        o = jnp.einsum("bhcd,bhde->bhce", qi, state)
        state = state * di[..., None] + jnp.einsum("bhcd,bhce->bhde", ki, vi)
        return state, o

    xs = (jnp.moveaxis(q_inter, 2, 0), jnp.moveaxis(k_inter, 2, 0),
          jnp.moveaxis(vc, 2, 0), jnp.moveaxis(chunk_decay, 2, 0))
    s0 = jnp.zeros((bsz, n_h, dk, dv), jnp.float32)
    _, o_inter = lax.scan(step, s0, xs)
    o = o_intra + jnp.moveaxis(o_inter, 0, 2)
    return o.transpose(0, 2, 3, 1, 4).reshape(bsz, t_len, n_h, dv)


def hybrid_layer(x, norm_w, w_in, gla_w_gate_up, gla_b_gate, gla_norm_w,
                 sc_conv_w, cf_conv_w, cf_conv_b, cf_ln_w, cf_ln_b, w_out):
    bsz, t_len, _ = x.shape
    h = rms_norm(x, norm_w)
    proj = h @ w_in
    (g_q, g_k, g_v, g_gate, g_lr,
     s_b, s_c, s_h, s_gate,
     c_a, c_b, c_gate) = jnp.split(proj, split_points(), axis=-1)

    gate_logits = (g_lr @ gla_w_gate_up + gla_b_gate).astype(jnp.float32)
    log_alpha = jax.nn.log_sigmoid(gate_logits) / GLA_GATE_NORMALIZER
    q = g_q.reshape(bsz, t_len, GLA_HEADS, GLA_DK)
    k = g_k.reshape(bsz, t_len, GLA_HEADS, GLA_DK)
    v = g_v.reshape(bsz, t_len, GLA_HEADS, GLA_DV)
    la = log_alpha.reshape(bsz, t_len, GLA_HEADS, GLA_DK)
    o = gla_chunked(q, k, v, la)
    o = o * lax.rsqrt(jnp.mean(o * o, axis=-1, keepdims=True) + NORM_EPS) * gla_norm_w.astype(jnp.float32)
    o_a = o.reshape(bsz, t_len, GLA_VWIDTH).astype(x.dtype) * jax.nn.silu(g_gate)

    o_b = s_b * causal_depthwise_conv(s_c * s_h, sc_conv_w) * jax.nn.silu(s_gate)

    c = c_a * jax.nn.sigmoid(c_b)
    c = causal_depthwise_conv(c, cf_conv_w, cf_conv_b)
    c = layer_norm(c, cf_ln_w, cf_ln_b)
    o_c = jax.nn.silu(c) * jax.nn.silu(c_gate)

    y = jnp.concatenate([o_a, o_b, o_c], axis=-1) @ w_out
    return x + y


def setup_inputs(seed: int = 0) -> dict:
    key = jax.random.key(seed)
    ks = jax.random.split(key, 14)
    f32 = jnp.float32
    x = jax.random.normal(ks[0], (BATCH, SEQ, D_MODEL), f32)
    norm_w = 1.0 + 0.01 * jax.random.normal(ks[1], (DEPTH, D_MODEL), f32)
    w_in = jax.random.normal(ks[2], (DEPTH, D_MODEL, D_IN), f32) * (D_MODEL ** -0.5)
    gla_w_gate_up = jax.random.normal(ks[3], (DEPTH, GLA_GATE_RANK, GLA_KWIDTH), f32) * (GLA_GATE_RANK ** -0.5)
    gla_b_gate = 0.1 * jax.random.normal(ks[4], (DEPTH, GLA_KWIDTH), f32)
    gla_norm_w = 1.0 + 0.01 * jax.random.normal(ks[5], (DEPTH, GLA_DV), f32)
    sc_conv_w = jax.random.normal(ks[6], (DEPTH, SC_CONV, SC_WIDTH), f32) * (SC_CONV ** -0.5)
    cf_conv_w = jax.random.normal(ks[7], (DEPTH, CF_CONV, CF_WIDTH), f32) * (CF_CONV ** -0.5)
    cf_conv_b = 0.01 * jax.random.normal(ks[8], (DEPTH, CF_WIDTH), f32)
    cf_ln_w = 1.0 + 0.01 * jax.random.normal(ks[9], (DEPTH, CF_WIDTH), f32)
    cf_ln_b = 0.01 * jax.random.normal(ks[10], (DEPTH, CF_WIDTH), f32)
    w_out = jax.random.normal(ks[11], (DEPTH, D_MIX, D_MODEL), f32) * (D_MIX ** -0.5)
    final_norm_w = 1.0 + 0.01 * jax.random.normal(ks[12], (D_MODEL,), f32)
    return {"x": x, "norm_w": norm_w, "w_in": w_in, "gla_w_gate_up": gla_w_gate_up,
            "gla_b_gate": gla_b_gate, "gla_norm_w": gla_norm_w, "sc_conv_w": sc_conv_w,
            "cf_conv_w": cf_conv_w, "cf_conv_b": cf_conv_b, "cf_ln_w": cf_ln_w,
            "cf_ln_b": cf_ln_b, "w_out": w_out, "final_norm_w": final_norm_w}


def reference(x, norm_w, w_in, gla_w_gate_up, gla_b_gate, gla_norm_w, sc_conv_w,
              cf_conv_w, cf_conv_b, cf_ln_w, cf_ln_b, w_out, final_norm_w):
    for layer in range(DEPTH):
        x = hybrid_layer(x, norm_w[layer], w_in[layer], gla_w_gate_up[layer], gla_b_gate[layer],
                         gla_norm_w[layer], sc_conv_w[layer], cf_conv_w[layer], cf_conv_b[layer],
                         cf_ln_w[layer], cf_ln_b[layer], w_out[layer])
    return rms_norm(x, final_norm_w)
```

```python
import math
from contextlib import ExitStack

import numpy as np
import ml_dtypes

import concourse.bass as bass
import concourse.mybir as mybir
from concourse.bass_utils import run_bass_kernel_spmd

F32 = mybir.dt.float32
BF16 = mybir.dt.bfloat16
AF = mybir.ActivationFunctionType
ALU = mybir.AluOpType

D = 1024
KC = 8
DIN = 6672
DMIX = 2048
NH = 4
DEPTH = 4
SEQ = 8192
NCORES = 8
TPG = 2
G = TPG * 128
NORM_EPS = 1e-6
LN_EPS = 1e-5
NU = 17
RING = 5

UCOL = {"q": 0, "k": 512, "v0": 1024, "v1": 1536, "gg0": 2048, "gg1": 2560,
        "sb": 3088, "sc": 3600, "sh": 4112, "sgate": 4624, "ca": 5136, "cb": 5648, "cgate": 6160}
LR_COL = 3072
UORDER = ["cb", "ca", "sc", "sh", "sgate", "sb", "cgate", "v0", "v1", "gg0", "gg1", "q", "k",
          "wo2", "wo3", "wo0", "wo1"]
UIDX = {n: i for i, n in enumerate(UORDER)}


class Res:
    __slots__ = ("w", "r")

    def __init__(self):
        self.w = None
        self.r = {}


class DSem:
    def __init__(self, sem):
        self.sem = sem
        self.cnt = 0


class Eng:
    def __init__(self, name, sem):
        self.name = name
        self.sem = sem
        self.cnt = 0
        self.seen = {}
        self.prog = []


class Prog:
    def __init__(self, nc, stack):
        self.nc = nc
        self.stack = stack
        self.E = {}
        for n in ("pe", "act", "dve", "pool", "sp"):
            self.E[n] = Eng(n, stack.enter_context(nc.semaphore("tl_" + n)))
        self.dsems = []

    def dsem(self, name):
        d = DSem(self.stack.enter_context(self.nc.semaphore(name)))
        self.dsems.append(d)
        return d

    def _wait(self, e, tok):
        sem, val, owner = tok
        k = id(sem)
        if e.seen.get(k, 0) >= val:
            return
        e.seen[k] = val
        e.prog.append(("w", sem, val))

    def _deps(self, e, reads, writes):
        for r in reads:
            if r.w is not None:
                t = r.w
                if not (t[2] is e and e.name == "pe"):
                    self._wait(e, t)
        for w in writes:
            if w.w is not None and w.w[2] is not e:
                self._wait(e, w.w)
            for t in w.r.values():
                if t[2] is not e:
                    self._wait(e, t)

    def _mark(self, tok, reads, writes):
        k = id(tok[0])
        for r in reads:
            r.r[k] = tok
        for w in writes:
            w.w = tok
            w.r = {}

    def op(self, en, fn, reads=(), writes=()):
        e = self.E[en]
        self._deps(e, reads, writes)
        e.cnt += 1
        tok = (e.sem, e.cnt, e)
        e.prog.append(("o", fn))
        self._mark(tok, reads, writes)
        return tok

    def dma(self, en, ds, out, in_, reads=(), writes=(), slow=False):
        e = self.E[en]
        self._deps(e, reads, writes)
        ds.cnt += 1
        tok = (ds.sem, 16 * ds.cnt, None)
        e.prog.append(("d", out, in_, ds.sem, slow))
        self._mark(tok, reads, writes)
        return tok

    def barrier(self):
        for e in self.E.values():
            for o in self.E.values():
                if o is not e and o.cnt > 0:
                    self._wait(e, (o.sem, o.cnt, o))
            for d in self.dsems:
                if d.cnt > 0:
                    self._wait(e, (d.sem, 16 * d.cnt, None))

    def finish(self, en):
        e = self.E[en]
        for d in self.dsems:
            if d.cnt > 0:
                self._wait(e, (d.sem, 16 * d.cnt, None))

    def emit(self):
        nc = self.nc

        def run(e, eng):
            for it in e.prog:
                if it[0] == "w":
                    eng.wait_ge(it[1], it[2])
                elif it[0] == "o":
                    it[1](eng).then_inc(e.sem, 1)
                else:
                    if it[4]:
                        eng.dma_start(out=it[1], in_=it[2], allow_slow_non_contiguous=True).then_inc(it[3], 16)
                    else:
                        eng.dma_start(out=it[1], in_=it[2]).then_inc(it[3], 16)

        with nc.Block() as blk:
            @blk.tensor
            def _(eng):
                run(self.E["pe"], eng)

            @blk.scalar
            def _(eng):
                run(self.E["act"], eng)

            @blk.vector
            def _(eng):
                run(self.E["dve"], eng)

            @blk.gpsimd
            def _(eng):
                run(self.E["pool"], eng)

            @blk.sync
            def _(eng):
                run(self.E["sp"], eng)


def build_program(T=SEQ, depth=DEPTH, debug=False):
    NT = T // 128
    NG = T // G
    nc = bass.Bass("TRN2", target_bir_lowering=False)
    stack = ExitStack()
    pg = Prog(nc, stack)

    def dram(name, shape, dt, kind):
        return nc.dram_tensor(name, list(shape), dt, kind=kind).ap()

    x_in = dram("x", [T, D], F32, "ExternalInput")
    norm_w = dram("norm_w", [depth, D], F32, "ExternalInput")
    w_in = dram("w_in", [depth, D, DIN], F32, "ExternalInput")
    w_gup = dram("gla_w_gate_up", [depth, 16, 512], F32, "ExternalInput")
    b_gate = dram("gla_b_gate", [depth, 512], F32, "ExternalInput")
    gnorm_w = dram("gla_norm_w", [depth, 256], F32, "ExternalInput")
    sc_w = dram("sc_conv_w", [depth, 3, 512], F32, "ExternalInput")
    cf_w = dram("cf_conv_w", [depth, 31, 512], F32, "ExternalInput")
    cf_b = dram("cf_conv_b", [depth, 512], F32, "ExternalInput")
    ln_w = dram("cf_ln_w", [depth, 512], F32, "ExternalInput")
    ln_b = dram("cf_ln_b", [depth, 512], F32, "ExternalInput")
    w_out = dram("w_out", [depth, DMIX, D], F32, "ExternalInput")
    fnorm_w = dram("final_norm_w", [1, D], F32, "ExternalInput")
    c_ident = dram("c_ident", [128, 128], BF16, "ExternalInput")
    c_mask = dram("c_mask", [128, 128], BF16, "ExternalInput")
    c_scan = dram("c_scan", [128, G], BF16, "ExternalInput")
    c_ones = dram("c_ones", [128, 128], BF16, "ExternalInput")
    out = dram("out", [T, D], F32, "ExternalOutput")
    dbg = dram("dbg", [128, 4096], BF16, "ExternalOutput") if debug else None
    xs = [dram("xs0", [T, D], F32, "Internal"), dram("xs1", [T, D], F32, "Internal")]
    wbf = dram("wbf", [depth * NU, 128, 4096], BF16, "Internal")

    def sb(name, shape, dt):
        return stack.enter_context(nc.sbuf_tensor(name, list(shape), dt))

    ident = sb("ident", [128, 128], BF16)
    maskt = sb("maskt", [128, 128], BF16)
    scanm = sb("scanm", [128, G], BF16)
    onesm = sb("onesm", [128, 128], BF16)
    fnw = sb("fnw", [128, D], F32)
    cst = sb("cst", [128, 4], F32)
    nwT = sb("nwT", [128, depth * 8], F32)
    gnwT = sb("gnwT", [128, depth * 2], F32)
    nbgT = sb("nbgT", [128, depth * 4], F32)
    cfbT = sb("cfbT", [128, depth * 4], F32)
    lnwT = sb("lnwT", [128, depth * 4], F32)
    lnbT = sb("lnbT", [128, depth * 4], F32)
    cfwT = sb("cfwT", [128, depth * 4, 31], F32)
    scwT = sb("scwT", [128, depth * 4, 3], F32)
    wlr = sb("wlr", [128, depth, 8, 16], BF16)
    wg = sb("wg", [16, 512], F32)
    dcf = sb("dcf", [128, 4, 31, 128], BF16)
    dsc = sb("dsc", [128, 4, 3, 128], BF16)
    ring = [sb(f"ring{i}", [128, 4096], BF16) for i in range(RING)]
    xpool = [sb(f"xp{i}", [128, D], F32) for i in range(4)]
    hbf = [sb(f"hbf{i}", [128, D], BF16) for i in range(2)]
    hT = sb("hT", [128, 8, G], BF16)
    NTMP = 6
    tmp = [sb(f"tmp{i}", [128, G], F32) for i in range(NTMP)]
    glu = [[sb(f"glu{p}{j}", [128, 30 + G], BF16) for j in range(4)] for p in range(2)]
    ubuf = [[sb(f"u{p}{j}", [128, 2 + G], BF16) for j in range(4)] for p in range(2)]
    cbf = [sb(f"cbf{j}", [128, G], BF16) for j in range(4)]
    csq = [sb(f"csq{j}", [128, G], BF16) for j in range(4)]
    mean_sb = sb("mean_sb", [128, G], F32)
    rstd_sb = sb("rstd_sb", [128, G], F32)
    t1 = [sb(f"t1{j}", [128, G], BF16) for j in range(4)]
    glr = sb("glr", [16, G], F32)
    dbuf = [sb(f"dbuf{h}", [128, G], F32) for h in range(4)]
    clast = sb("clast", [128, 4, TPG], F32)
    dec = sb("dec", [128, 4, TPG], F32)
    qi = [sb(f"qi{h}", [128, G], BF16) for h in range(4)]
    ki = [sb(f"ki{h}", [128, G], BF16) for h in range(4)]
    kiT = [sb(f"kiT{t}", [128, 512], BF16) for t in range(TPG)]
    arena = sb("arena", [128, 8192], BF16)
    omix = arena[:, 0:16 * G].rearrange("p (m c) -> p m c", c=G)
    v_sb = [arena[:, 4096 + t * 1024:4096 + (t + 1) * 1024] for t in range(TPG)]
    gsil = [arena[:, 6144 + t * 1024:6144 + (t + 1) * 1024] for t in range(TPG)]
    PTs = [sb(f"PTs{i}", [128, 4, 128], BF16) for i in range(2)]
    S = sb("S", [128, 4, 256], F32)
    Dbf = [sb(f"Dbf{i}", [128, 4, 256], BF16) for i in range(2)]
    on = [sb(f"on{i}", [128, D], BF16) for i in range(2)]
    junk = sb("junk", [128, D], BF16)
    ss = sb("ss", [128, 8], F32)
    rs = sb("rs", [128, 8], F32)
    ss4 = sb("ss4", [128, 8], F32)
    rs4 = sb("rs4", [128, 8], F32)
    stg = [sb(f"stg{i}", [128, 4096], F32) for i in range(2)]
    sbo = [arena[:, i * 4096:(i + 1) * 4096] for i in range(2)]

    ps = [stack.enter_context(nc.psum_tensor(f"ps{i}", [128, 512], F32)) for i in range(8)]
    ps_r = [Res() for _ in range(8)]
    NFB = 6
    fb = [0]
    tb = [0]

    def bank():
        b = fb[0] % NFB
        fb[0] += 1
        return b

    def tbank():
        b = NFB + (tb[0] % 2)
        tb[0] += 1
        return b

    R = {}

    def res(key):
        if key not in R:
            R[key] = Res()
        return R[key]

    setup_sem = pg.dsem("d_setup")
    stg_sem = [pg.dsem(f"d_stg{i}") for i in range(2)]
    sbo_sem = [pg.dsem(f"d_sbo{i}") for i in range(2)]
    ring_sem = [pg.dsem(f"d_ring{i}") for i in range(RING)]
    xp_sem = [pg.dsem(f"d_xp{i}") for i in range(4)]
    xo_sem = [pg.dsem(f"d_xo{i}") for i in range(4)]
    wg_sem = pg.dsem("d_wg")

    def act(fn, reads, writes):
        return pg.op("act", fn, reads, writes)

    def dve(fn, reads, writes):
        return pg.op("dve", fn, reads, writes)

    def pool(fn, reads, writes):
        return pg.op("pool", fn, reads, writes)

    def pe(fn, reads, writes):
        return pg.op("pe", fn, reads, writes)

    LNSCALE = float(-0.5 * math.log(128.0))
    consts_r = res("consts")

    def setup_dma(o, i, slow=False):
        pg.dma("sp", setup_sem, o, i, reads=(), writes=(consts_r,), slow=slow)

    setup_dma(ident[:], c_ident)
    setup_dma(maskt[:], c_mask)
    setup_dma(scanm[:], c_scan)
    setup_dma(onesm[:], c_ones)
    setup_dma(fnw[:], fnorm_w.partition_broadcast(128))
    setup_dma(nwT[:], norm_w.rearrange("l (k p) -> p (l k)", p=128), slow=True)
    setup_dma(gnwT[:], gnorm_w.rearrange("l (k p) -> p (l k)", p=128), slow=True)
    setup_dma(nbgT[:], b_gate.rearrange("l (k p) -> p (l k)", p=128), slow=True)
    setup_dma(cfbT[:], cf_b.rearrange("l (k p) -> p (l k)", p=128), slow=True)
    setup_dma(lnwT[:], ln_w.rearrange("l (k p) -> p (l k)", p=128), slow=True)
    setup_dma(lnbT[:], ln_b.rearrange("l (k p) -> p (l k)", p=128), slow=True)
    for l in range(depth):
        for j in range(4):
            setup_dma(cfwT[:, l * 4 + j, :], cf_w[l, :, j * 128:(j + 1) * 128].rearrange("k p -> p k"), slow=True)
            setup_dma(scwT[:, l * 4 + j, :], sc_w[l, :, j * 128:(j + 1) * 128].rearrange("k p -> p k"), slow=True)
    pool(lambda e: e.memset(cst[:, 0:1], float(NORM_EPS)), (), (consts_r,))
    pool(lambda e: e.memset(cst[:, 1:2], float(LN_EPS)), (), (consts_r,))
    pool(lambda e: e.memset(cst[:, 2:3], LNSCALE), (), (consts_r,))
    pool(lambda e: e.memset(cst[:, 3:4], 1.0), (), (consts_r,))
    pg.barrier()
    dve(lambda e: e.tensor_scalar(out=nbgT[:], in0=nbgT[:], scalar1=-1.0, scalar2=None, op0=ALU.mult),
        (consts_r,), (consts_r,))

    cvt_engs = ["act", "dve"]
    cvt_i = [0]

    def convert(si, views_in, views_out, scalars):
        for vin, vout, sc in zip(views_in, views_out, scalars):
            en = cvt_engs[cvt_i[0] % 2]
            cvt_i[0] += 1
            rd = (res(("stg", si)), consts_r)
            wr = (res(("sbo", si)),)
            if sc is None:
                if en == "act":
                    act(lambda e, a=vin, b=vout: e.copy(out=b, in_=a), rd, wr)
                else:
                    pg.op(en, lambda e, a=vin, b=vout: e.tensor_copy(out=b, in_=a), rd, wr)
            else:
                if en == "act":
                    act(lambda e, a=vin, b=vout, s=sc: e.activation(out=b, in_=a, func=AF.Copy, scale=s), rd, wr)
                else:
                    pg.op(en, lambda e, a=vin, b=vout, s=sc: e.tensor_scalar(
                        out=b, in0=a, scalar1=s, scalar2=None, op0=ALU.mult), rd, wr)

    cv = [0]
    for l in range(depth):
        for u, name in enumerate(UORDER):
            si = cv[0] % 2
            cv[0] += 1
            if name.startswith("wo"):
                i = int(name[2])
                src = w_out[l, i * 512:(i + 1) * 512, :].rearrange("(m p) c -> p m c", p=128)
                dst = stg[si][:].rearrange("p (m c) -> p m c", c=1024)
            else:
                c0 = UCOL[name]
                src = w_in[l, :, c0:c0 + 512].rearrange("(k p) c -> p k c", p=128)
                dst = stg[si][:].rearrange("p (k c) -> p k c", c=512)
            pg.dma("sp", stg_sem[si], dst, src, reads=(), writes=(res(("stg", si)),))
            vin, vout, scs = [], [], []
            if name.startswith("wo"):
                i = int(name[2])
                for m in range(4):
                    for hf in range(2):
                        a = m * 1024 + hf * 512
                        vin.append(stg[si][:, a:a + 512])
                        vout.append(sbo[si][:, a:a + 512])
                        scs.append(gnwT[:, l * 2 + (m % 2):l * 2 + (m % 2) + 1] if i < 2 else None)
            else:
                for k in range(8):
                    vin.append(stg[si][:, k * 512:(k + 1) * 512])
                    vout.append(sbo[si][:, k * 512:(k + 1) * 512])
                    scs.append(nwT[:, l * 8 + k:l * 8 + k + 1])
            convert(si, vin, vout, scs)
            pg.dma("pool", sbo_sem[si], wbf[l * NU + u], sbo[si], reads=(res(("sbo", si)),),
                   writes=(res(("wbf", l, u)),))
        si = cv[0] % 2
        cv[0] += 1
        src = w_in[l, :, LR_COL:LR_COL + 16].rearrange("(k p) c -> p k c", p=128)
        dst = stg[si][:, 0:128].rearrange("p (k c) -> p k c", c=16)
        pg.dma("sp", stg_sem[si], dst, src, reads=(), writes=(res(("stg", si)),), slow=True)
        for k in range(8):
            dve(lambda e, a=stg[si][:, k * 16:(k + 1) * 16], b=wlr[:, l, k, :],
                s=nwT[:, l * 8 + k:l * 8 + k + 1]: e.tensor_scalar(out=b, in0=a, scalar1=s, scalar2=None,
                                                                   op0=ALU.mult),
                (res(("stg", si)), consts_r), (res("wlr"),))
    pg.barrier()

    useq = []
    for l in range(depth):
        for g in range(NG):
            for name in UORDER:
                useq.append((l, g, name))
    loaded = [0]

    def ensure_loaded(upto):
        while loaded[0] <= upto and loaded[0] < len(useq):
            i = loaded[0]
            l, g, name = useq[i]
            s = i % RING
            pg.dma("sp", ring_sem[s], ring[s][:], wbf[l * NU + UIDX[name]],
                   reads=(res(("wbf", l, UIDX[name])),), writes=(res(("ring", s)),))
            loaded[0] += 1

    upos = [0]

    def next_unit(l, g, name, hold=0):
        i = upos[0]
        assert useq[i] == (l, g, name), (useq[i], l, g, name)
        upos[0] += 1
        ensure_loaded(i + RING - 1 - hold)
        s = i % RING
        return ring[s], res(("ring", s))

    tmp_i = [0]

    def gettmp():
        i = tmp_i[0] % NTMP
        tmp_i[0] += 1
        return tmp[i], res(("tmp", i))

    hT_r = [res(("hT", t)) for t in range(TPG)]

    def Xsrc(l):
        return x_in if l == 0 else xs[(l - 1) % 2]

    def Xdst(l):
        return out if l == depth - 1 else xs[l % 2]

    def Xres(l, n):
        return res(("X", l, n))

    def layer_setup(l):
        pg.dma("sp", wg_sem, wg[:], w_gup[l], reads=(), writes=(res("wg"),))
        for j in range(4):
            for k in range(31):
                en = "dve"
                pg.op(en, lambda e, o=dcf[:, j, k, :], s=cfwT[:, l * 4 + j, k:k + 1]: e.tensor_scalar(
                    out=o, in0=ident[:], scalar1=s, scalar2=None, op0=ALU.mult),
                    (consts_r,), (res("dcf"),))
            for k in range(3):
                dve(lambda e, o=dsc[:, j, k, :], s=scwT[:, l * 4 + j, k:k + 1]: e.tensor_scalar(
                    out=o, in0=ident[:], scalar1=s, scalar2=None, op0=ALU.mult),
                    (consts_r,), (res("dsc"),))
        pool(lambda e: e.memset(S[:], 0.0), (), (res("S"),))

    xslot = [0]

    def norm_a(l, g):
        for tt in range(TPG):
            n = g * TPG + tt
            s = xslot[0] % 4
            xslot[0] += 1
            hs = n % 2
            xr = res(("xp", s))
            pg.dma("sp", xp_sem[s], xpool[s][:], Xsrc(l)[n * 128:(n + 1) * 128, :],
                   reads=(Xres(l, n),), writes=(xr,))
            ssr = res(("ss", hs))
            act(lambda e, s=s, hs=hs: e.activation(out=junk[:], in_=xpool[s][:], func=AF.Square,
                                                   accum_out=ss[:, hs:hs + 1]),
                (xr,), (ssr,))
            rsr = res(("rs", hs))
            act(lambda e, hs=hs: e.activation(out=rs[:, hs:hs + 1], in_=ss[:, hs:hs + 1], func=AF.Ln,
                                              bias=cst[:, 0:1], scale=1.0 / D),
                (ssr, consts_r), (rsr,))
            act(lambda e, hs=hs: e.activation(out=rs[:, hs:hs + 1], in_=rs[:, hs:hs + 1], func=AF.Exp,
                                              scale=-0.5),
                (rsr,), (rsr,))
            hr = res(("hbf", hs))
            act(lambda e, s=s, hs=hs: e.activation(out=hbf[hs][:], in_=xpool[s][:], func=AF.Copy,
                                                   scale=rs[:, hs:hs + 1]),
                (xr, rsr), (hr,))

    def norm_b(l, g):
        for tt in range(TPG):
            n = g * TPG + tt
            hs = n % 2
            hr = res(("hbf", hs))
            b = tbank()
            pv = ps[b].bitcast(BF16)

            def tr(e, hs=hs, pv=pv):
                ins = None
                for k in range(8):
                    ins = e.transpose(out=pv[:, k * 128:(k + 1) * 128], in_=hbf[hs][:, k * 128:(k + 1) * 128],
                                      identity=ident[:])
                return ins
            pe(tr, (hr, consts_r), (ps_r[b],))
            act(lambda e, pv=pv, tt=tt: e.copy(out=hT[:, :, tt * 128:(tt + 1) * 128],
                                               in_=pv[:, 0:1024].rearrange("p (k c) -> p k c", c=128)),
                (ps_r[b],), (hT_r[tt],))

    def fm_chunk(W, Wr, j, M=128):
        b = bank()

        def mm(e, b=b, j=j):
            ins = None
            for k in range(8):
                ins = e.matmul(ps[b][0:M, 0:G], lhsT=W[:, k * 512 + j * 128:k * 512 + j * 128 + M],
                               rhs=hT[:, k, :], start=(k == 0), stop=(k == 7))
            return ins
        pe(mm, (Wr,) + tuple(hT_r), (ps_r[b],))
        return b

    def tm_tile(W, Wr, tt):
        b = bank()

        def mm(e, b=b, tt=tt):
            ins = None
            for k in range(8):
                ins = e.matmul(ps[b][:, :], lhsT=hT[:, k, tt * 128:(tt + 1) * 128],
                               rhs=W[:, k * 512:(k + 1) * 512], start=(k == 0), stop=(k == 7))
            return ins
        pe(mm, (Wr, hT_r[tt]), (ps_r[b],))
        return b


    def gates(l, g):
        b = bank()

        def mm_lr(e, b=b):
            ins = None
            for k in range(8):
                ins = e.matmul(ps[b][0:16, 0:G], lhsT=wlr[:, l, k, :], rhs=hT[:, k, :],
                               start=(k == 0), stop=(k == 7))
            return ins
        pe(mm_lr, (res("wlr"),) + tuple(hT_r), (ps_r[b],))
        glr_r = res("glr")
        act(lambda e, b=b: e.copy(out=glr[:], in_=ps[b][0:16, 0:G]), (ps_r[b],), (glr_r,))
        for h in range(4):
            b = bank()
            pe(lambda e, b=b, h=h: e.matmul(ps[b][:, 0:G], lhsT=wg[:, h * 128:(h + 1) * 128], rhs=glr[:],
                                            start=True, stop=True),
               (res("wg"), glr_r), (ps_r[b],))
            t_e, t_er = gettmp()
            act(lambda e, b=b, h=h, t=t_e: e.activation(out=t[:], in_=ps[b][:, 0:G], func=AF.Exp,
                                                        bias=nbgT[:, l * 4 + h:l * 4 + h + 1], scale=-1.0),
                (ps_r[b], consts_r), (t_er,))
            act(lambda e, t=t_e: e.activation(out=t[:], in_=t[:], func=AF.Ln, bias=cst[:, 3:4], scale=1.0),
                (t_er,), (t_er,))
            dr = res(("d", h))
            dve(lambda e, t=t_e, h=h: e.tensor_tensor_scan(out=dbuf[h][:], data0=scanm[:], data1=t[:],
                                                           initial=0.0, op0=ALU.mult, op1=ALU.add),
                (t_er, consts_r), (dr,))
            clr = res(("cl", h))
            dve(lambda e, h=h: e.tensor_copy(
                out=clast[:, h, :], in_=dbuf[h][:].rearrange("p (t c) -> p t c", c=128)[:, :, 127]),
                (dr,), (clr,))
            dve(lambda e, h=h: e.tensor_tensor(
                out=dbuf[h][:].rearrange("p (t c) -> p t c", c=128),
                in0=dbuf[h][:].rearrange("p (t c) -> p t c", c=128),
                in1=clast[:, h, :].unsqueeze(2).to_broadcast([128, TPG, 128]), op=ALU.subtract),
                (dr, clr), (dr,))
            act(lambda e, h=h: e.activation(out=dec[:, h, :], in_=clast[:, h, :], func=AF.Exp,
                                            scale=-1.0 / 16.0),
                (clr,), (res(("dec", h)),))


    def group(l, g):
        par = g % 2
        last_layer = (l == depth - 1)
        Wb, Wbr = next_unit(l, g, "cb")
        Wa, War = next_unit(l, g, "ca", hold=1)
        for j in range(4):
            gr = res(("glu", par, j))
            if g == 0:
                pool(lambda e, j=j: e.memset(glu[par][j][:, 0:30], 0.0), (), (gr,))
            else:
                pool(lambda e, j=j: e.tensor_copy(out=glu[par][j][:, 0:30], in_=glu[1 - par][j][:, G:G + 30]),
                     (res(("glu", 1 - par, j)),), (gr,))
            bb = fm_chunk(Wb, Wbr, j)
            t_s, t_sr = gettmp()
            act(lambda e, b=bb, t=t_s: e.activation(out=t[:], in_=ps[b][:, 0:G], func=AF.Sigmoid),
                (ps_r[bb],), (t_sr,))
            ba = fm_chunk(Wa, War, j)
            dve(lambda e, b=ba, t=t_s, j=j: e.tensor_tensor(out=glu[par][j][:, 30:30 + G], in0=ps[b][:, 0:G],
                                                            in1=t[:], op=ALU.mult),
                (ps_r[ba], t_sr), (gr,))
        Wc, Wcr = next_unit(l, g, "sc")
        Wh, Whr = next_unit(l, g, "sh", hold=1)
        for j in range(4):
            ur = res(("u", par, j))
            if g == 0:
                pool(lambda e, j=j: e.memset(ubuf[par][j][:, 0:2], 0.0), (), (ur,))
            else:
                pool(lambda e, j=j: e.tensor_copy(out=ubuf[par][j][:, 0:2], in_=ubuf[1 - par][j][:, G:G + 2]),
                     (res(("u", 1 - par, j)),), (ur,))
            bc = fm_chunk(Wc, Wcr, j)
            t_s, t_sr = gettmp()
            dve(lambda e, b=bc, t=t_s: e.tensor_copy(out=t[:], in_=ps[b][:, 0:G]), (ps_r[bc],), (t_sr,))
            bh = fm_chunk(Wh, Whr, j)
            dve(lambda e, b=bh, t=t_s, j=j: e.tensor_tensor(out=ubuf[par][j][:, 2:2 + G], in0=ps[b][:, 0:G],
                                                            in1=t[:], op=ALU.mult),
                (ps_r[bh], t_sr), (ur,))
        for j in range(4):
            b = bank()

            def mmc(e, b=b, j=j):
                ins = None
                for k in range(31):
                    ins = e.matmul(ps[b][:, 0:G], lhsT=dcf[:, j, k, :], rhs=glu[par][j][:, k:k + G],
                                   start=(k == 0), stop=(k == 30))
                return ins
            pe(mmc, (res("dcf"), res(("glu", par, j))), (ps_r[b],))
            act(lambda e, b=b, j=j: e.activation(out=cbf[j][:], in_=ps[b][:, 0:G], func=AF.Identity,
                                                 bias=cfbT[:, l * 4 + j:l * 4 + j + 1], scale=1.0),
                (ps_r[b], consts_r), (res(("cbf", j)),))
            act(lambda e, b=b, j=j: e.activation(out=csq[j][:], in_=ps[b][:, 0:G], func=AF.Square,
                                                 bias=cfbT[:, l * 4 + j:l * 4 + j + 1], scale=1.0),
                (ps_r[b], consts_r), (res(("csq", j)),))
        Wsg, Wsgr = next_unit(l, g, "sgate")
        Wsb, Wsbr = next_unit(l, g, "sb", hold=1)
        for j in range(4):
            bg_ = fm_chunk(Wsg, Wsgr, j)
            t_s, t_sr = gettmp()
            act(lambda e, b=bg_, t=t_s: e.activation(out=t[:], in_=ps[b][:, 0:G], func=AF.Silu),
                (ps_r[bg_],), (t_sr,))
            bs = fm_chunk(Wsb, Wsbr, j)
            dve(lambda e, b=bs, t=t_s, j=j: e.tensor_tensor(out=t1[j][:], in0=ps[b][:, 0:G], in1=t[:],
                                                            op=ALU.mult),
                (ps_r[bs], t_sr), (res(("t1", j)),))
        for j in range(4):
            b = bank()

            def mms(e, b=b, j=j):
                ins = None
                for k in range(3):
                    ins = e.matmul(ps[b][:, 0:G], lhsT=dsc[:, j, k, :], rhs=ubuf[par][j][:, k:k + G],
                                   start=(k == 0), stop=(k == 2))
                return ins
            pe(mms, (res("dsc"), res(("u", par, j))), (ps_r[b],))
            dve(lambda e, b=b, j=j: e.tensor_tensor(out=omix[:, 8 + j, :], in0=ps[b][:, 0:G], in1=t1[j][:],
                                                    op=ALU.mult),
                (ps_r[b], res(("t1", j))), (res(("omix", 8 + j)),))
        b1 = bank()

        def mm_mean(e, b=b1):
            ins = None
            for j in range(4):
                ins = e.matmul(ps[b][:, 0:G], lhsT=onesm[:], rhs=cbf[j][:], start=(j == 0), stop=(j == 3))
            return ins
        pe(mm_mean, tuple(res(("cbf", j)) for j in range(4)) + (consts_r,), (ps_r[b1],))
        b2 = bank()

        def mm_msq(e, b=b2):
            ins = None
            for j in range(4):
                ins = e.matmul(ps[b][:, 0:G], lhsT=onesm[:], rhs=csq[j][:], start=(j == 0), stop=(j == 3))
            return ins
        pe(mm_msq, tuple(res(("csq", j)) for j in range(4)) + (consts_r,), (ps_r[b2],))
        mr = res("mean")
        rr = res("rstd")
        act(lambda e, b=b1: e.copy(out=mean_sb[:], in_=ps[b][:, 0:G]), (ps_r[b1],), (mr,))
        t_m, t_mr = gettmp()
        dve(lambda e, t=t_m: e.tensor_tensor(out=t[:], in0=mean_sb[:], in1=mean_sb[:], op=ALU.mult),
            (mr,), (t_mr,))
        dve(lambda e, t=t_m, b=b2: e.tensor_tensor(out=t[:], in0=ps[b][:, 0:G], in1=t[:], op=ALU.subtract),
            (ps_r[b2], t_mr), (t_mr,))
        dve(lambda e, t=t_m: e.tensor_scalar(out=t[:], in0=t[:], scalar1=0.0, scalar2=None, op0=ALU.max),
            (t_mr,), (t_mr,))
        act(lambda e, t=t_m: e.activation(out=t[:], in_=t[:], func=AF.Ln, bias=cst[:, 1:2], scale=1.0),
            (t_mr, consts_r), (t_mr,))
        act(lambda e, t=t_m: e.activation(out=rstd_sb[:], in_=t[:], func=AF.Exp, scale=-0.5),
            (t_mr,), (rr,))
        for j in range(4):
            t_y, t_yr = gettmp()
            pool(lambda e, t=t_y, j=j: e.tensor_tensor(out=t[:], in0=cbf[j][:], in1=mean_sb[:],
                                                       op=ALU.subtract),
                 (res(("cbf", j)), mr), (t_yr,))
            pool(lambda e, t=t_y: e.tensor_tensor(out=t[:], in0=t[:], in1=rstd_sb[:], op=ALU.mult),
                 (t_yr, rr), (t_yr,))
            act(lambda e, t=t_y, j=j: e.activation(out=omix[:, 12 + j, :], in_=t[:], func=AF.Silu,
                                                   bias=lnbT[:, l * 4 + j:l * 4 + j + 1],
                                                   scale=lnwT[:, l * 4 + j:l * 4 + j + 1]),
                (t_yr, consts_r), (res(("omix", 12 + j)),))
        W, Wr = next_unit(l, g, "cgate")
        for j in range(4):
            b = fm_chunk(W, Wr, j)
            t_s, t_sr = gettmp()
            act(lambda e, b=b, t=t_s: e.activation(out=t[:], in_=ps[b][:, 0:G], func=AF.Silu),
                (ps_r[b],), (t_sr,))
            pool(lambda e, t=t_s, j=j: e.tensor_tensor(out=omix[:, 12 + j, :], in0=omix[:, 12 + j, :], in1=t[:],
                                                       op=ALU.mult),
                 (t_sr, res(("omix", 12 + j))), (res(("omix", 12 + j)),))
        if g + 1 < NG:
            norm_a(l, g + 1)
        for hf, name in enumerate(("v0", "v1")):
            W, Wr = next_unit(l, g, name)
            for tt in range(TPG):
                b = tm_tile(W, Wr, tt)
                dve(lambda e, b=b, tt=tt, hf=hf: e.tensor_copy(out=v_sb[tt][:, hf * 512:(hf + 1) * 512],
                                                               in_=ps[b][:, :]),
                    (ps_r[b],), (res(("v", tt)),))
        for hf, name in enumerate(("gg0", "gg1")):
            W, Wr = next_unit(l, g, name)
            for tt in range(TPG):
                b = tm_tile(W, Wr, tt)
                act(lambda e, b=b, tt=tt, hf=hf: e.activation(out=gsil[tt][:, hf * 512:(hf + 1) * 512],
                                                              in_=ps[b][:, :], func=AF.Silu),
                    (ps_r[b],), (res(("gsil", tt)),))
        for name, dst, sc_, bi_ in (("q", qi, -1.0 / 16.0, cst[:, 2:3]), ("k", ki, 1.0 / 16.0, 0.0)):
            W, Wr = next_unit(l, g, name)
            facs = []
            for h in range(4):
                t_f, t_fr = gettmp()
                act(lambda e, t=t_f, h=h, sc_=sc_, bi_=bi_: e.activation(out=t[:], in_=dbuf[h][:], func=AF.Exp,
                                                                         bias=bi_, scale=sc_),
                    (res(("d", h)),), (t_fr,))
                facs.append((t_f, t_fr))
            for h in range(4):
                b = fm_chunk(W, Wr, h)
                t_f, t_fr = facs[h]
                dve(lambda e, b=b, t=t_f, h=h, dst=dst: e.tensor_tensor(out=dst[h][:], in0=ps[b][:, 0:G],
                                                                        in1=t[:], op=ALU.mult),
                    (ps_r[b], t_fr), (res((name + "i", h)),))
        if g + 1 < NG:
            norm_b(l, g + 1)
        def tinfo(tt):
            n = g * TPG + tt
            return n % 2, slice(tt * 128, (tt + 1) * 128)

        def e_kiT(tt):
            b = tbank()
            pv = ps[b].bitcast(BF16)
            x2, tsl = tinfo(tt)

            def trk(e, pv=pv, tsl=tsl):
                ins = None
                for h in range(4):
                    ins = e.transpose(out=pv[:, h * 128:(h + 1) * 128], in_=ki[h][:, tsl], identity=ident[:])
                return ins
            pe(trk, tuple(res(("ki", h)) for h in range(4)) + (consts_r,), (ps_r[b],))
            dve(lambda e, pv=pv, tt=tt: e.tensor_copy(out=kiT[tt][:], in_=pv[:, 0:512]), (ps_r[b],),
                (res(("kiT", tt)),))

        def e_PT(tt):
            x2, tsl = tinfo(tt)
            b = bank()

            def mmp(e, b=b, tsl=tsl):
                ins = None
                for h in range(4):
                    ins = e.matmul(ps[b][:, h * 128:(h + 1) * 128], lhsT=ki[h][:, tsl], rhs=qi[h][:, tsl],
                                   start=True, stop=True)
                return ins
            pe(mmp, tuple(res(("ki", h)) for h in range(4)) + tuple(res(("qi", h)) for h in range(4)),
               (ps_r[b],))
            dve(lambda e, b=b, x2=x2: e.tensor_tensor(
                out=PTs[x2][:], in0=ps[b][:, :].rearrange("p (h c) -> p h c", c=128),
                in1=maskt[:].unsqueeze(1).to_broadcast([128, 4, 128]), op=ALU.mult),
                (ps_r[b], consts_r), (res(("PTs", x2)),))

        def e_state(tt):
            x2, tsl = tinfo(tt)
            dbr = res(("Dbf", x2))
            for h in range(4):
                act(lambda e, h=h, x2=x2, tt=tt: e.activation(out=Dbf[x2][:, h, :], in_=S[:, h, :], func=AF.Copy,
                                                              scale=dec[:, h, tt:tt + 1]),
                    (res("S"), res(("dec", h))), (dbr,))
            bkv = [bank(), bank()]
            for hp in range(2):
                def mmkv(e, b=bkv[hp], hp=hp, tt=tt):
                    ins = None
                    for hh in range(2):
                        h = hp * 2 + hh
                        ins = e.matmul(ps[b][:, hh * 256:(hh + 1) * 256], lhsT=kiT[tt][:, h * 128:(h + 1) * 128],
                                       rhs=v_sb[tt][:, h * 256:(h + 1) * 256], start=True, stop=True)
                    return ins
                pe(mmkv, (res(("kiT", tt)), res(("v", tt))), (ps_r[bkv[hp]],))
            for h in range(4):
                b = bkv[h // 2]
                dve(lambda e, b=b, h=h, tt=tt: e.scalar_tensor_tensor(
                    out=S[:, h, :], in0=S[:, h, :], scalar=dec[:, h, tt:tt + 1],
                    in1=ps[b][:, (h % 2) * 256:(h % 2 + 1) * 256], op0=ALU.mult, op1=ALU.add),
                    (res("S"), res(("dec", h)), ps_r[b]), (res("S"),))

        def e_o(tt):
            x2, tsl = tinfo(tt)
            bo = [bank(), bank()]
            for hp in range(2):
                def mmo(e, b=bo[hp], hp=hp, tt=tt, x2=x2, tsl=tsl):
                    ins = None
                    for hh in range(2):
                        h = hp * 2 + hh
                        e.matmul(ps[b][:, hh * 256:(hh + 1) * 256], lhsT=PTs[x2][:, h, :],
                                 rhs=v_sb[tt][:, h * 256:(h + 1) * 256], start=True, stop=False)
                        ins = e.matmul(ps[b][:, hh * 256:(hh + 1) * 256], lhsT=qi[h][:, tsl],
                                       rhs=Dbf[x2][:, h, :], start=False, stop=True)
                    return ins
                pe(mmo, (res(("PTs", x2)), res(("v", tt)), res(("Dbf", x2))) +
                   tuple(res(("qi", h)) for h in range(4)), (ps_r[bo[hp]],))
            s4r = res(("ss4", x2))
            for h in range(4):
                b = bo[h // 2]
                act(lambda e, b=b, h=h, x2=x2: e.activation(
                    out=junk[:, 0:256], in_=ps[b][:, (h % 2) * 256:(h % 2 + 1) * 256], func=AF.Square,
                    accum_out=ss4[:, x2 * 4 + h:x2 * 4 + h + 1]),
                    (ps_r[b],), (s4r,))
            r4r = res(("rs4", x2))
            act(lambda e, x2=x2: e.activation(out=rs4[:, x2 * 4:x2 * 4 + 4], in_=ss4[:, x2 * 4:x2 * 4 + 4],
                                              func=AF.Ln, bias=cst[:, 0:1], scale=1.0 / 256.0),
                (s4r, consts_r), (r4r,))
            act(lambda e, x2=x2: e.activation(out=rs4[:, x2 * 4:x2 * 4 + 4], in_=rs4[:, x2 * 4:x2 * 4 + 4],
                                              func=AF.Exp, scale=-0.5),
                (r4r,), (r4r,))
            onr = res(("on", x2))
            for h in range(4):
                b = bo[h // 2]
                dve(lambda e, b=b, h=h, x2=x2, tt=tt: e.scalar_tensor_tensor(
                    out=on[x2][:, h * 256:(h + 1) * 256], in0=ps[b][:, (h % 2) * 256:(h % 2 + 1) * 256],
                    scalar=rs4[:, x2 * 4 + h:x2 * 4 + h + 1], in1=gsil[tt][:, h * 256:(h + 1) * 256],
                    op0=ALU.mult, op1=ALU.mult),
                    (ps_r[b], r4r, res(("gsil", tt))), (onr,))

        def e_oaT(tt):
            x2, tsl = tinfo(tt)
            b = tbank()
            pv = ps[b].bitcast(BF16)

            def tro(e, pv=pv, x2=x2):
                ins = None
                for c in range(8):
                    ins = e.transpose(out=pv[:, c * 128:(c + 1) * 128], in_=on[x2][:, c * 128:(c + 1) * 128],
                                      identity=ident[:])
                return ins
            pe(tro, (res(("on", x2)), consts_r), (ps_r[b],))
            act(lambda e, pv=pv, tsl=tsl: e.copy(out=omix[:, 0:8, tsl],
                                                 in_=pv[:, 0:1024].rearrange("p (k c) -> p k c", c=128)),
                (ps_r[b],), tuple(res(("omix", c)) for c in range(8)))

        for tt in range(TPG):
            e_kiT(tt)
        for tt in range(TPG):
            e_PT(tt)
        for tt in range(TPG):
            e_state(tt)
        for tt in range(TPG):
            e_o(tt)
        if g + 1 < NG:
            gates(l, g + 1)
        for tt in range(TPG):
            e_oaT(tt)
        return

    def wout_stage(l, g):
        last_layer = (l == depth - 1)
        yb = [[bank(), bank()] for _ in range(TPG)]
        for io, i in enumerate((2, 3, 0, 1)):
            W, Wr = next_unit(l, g, f"wo{i}")
            for tt in range(TPG):
                for hf in range(2):
                    b = yb[tt][hf]

                    def mmy(e, b=b, i=i, io=io, tt=tt, hf=hf, W=W):
                        ins = None
                        for m in range(4):
                            mc = i * 4 + m
                            ins = e.matmul(ps[b][:, :], lhsT=omix[:, mc, tt * 128:(tt + 1) * 128],
                                           rhs=W[:, m * 1024 + hf * 512:m * 1024 + (hf + 1) * 512],
                                           start=(io == 0 and m == 0), stop=(io == 3 and m == 3))
                        return ins
                    pe(mmy, (Wr,) + tuple(res(("omix", i * 4 + m)) for m in range(4)), (ps_r[b],))
        for tt in range(TPG):
            n = g * TPG + tt
            s = xslot[0] % 4
            xslot[0] += 1
            xr = res(("xp", s))
            pg.dma("sp", xp_sem[s], xpool[s][:], Xsrc(l)[n * 128:(n + 1) * 128, :],
                   reads=(Xres(l, n),), writes=(xr,))
            for hf in range(2):
                b = yb[tt][hf]
                dve(lambda e, b=b, s=s, hf=hf: e.tensor_tensor(out=xpool[s][:, hf * 512:(hf + 1) * 512],
                                                               in0=ps[b][:, :],
                                                               in1=xpool[s][:, hf * 512:(hf + 1) * 512],
                                                               op=ALU.add),
                    (ps_r[b], xr), (xr,))
            if last_layer:
                hs = 4 + (n % 2)
                ssr = res(("ss", hs))
                act(lambda e, s=s, hs=hs: e.activation(out=junk[:], in_=xpool[s][:], func=AF.Square,
                                                       accum_out=ss[:, hs:hs + 1]),
                    (xr,), (ssr,))
                rsr = res(("rs", hs))
                act(lambda e, hs=hs: e.activation(out=rs[:, hs:hs + 1], in_=ss[:, hs:hs + 1], func=AF.Ln,
                                                  bias=cst[:, 0:1], scale=1.0 / D),
                    (ssr, consts_r), (rsr,))
                act(lambda e, hs=hs: e.activation(out=rs[:, hs:hs + 1], in_=rs[:, hs:hs + 1], func=AF.Exp,
                                                  scale=-0.5),
                    (rsr,), (rsr,))
                dve(lambda e, s=s, hs=hs: e.scalar_tensor_tensor(out=xpool[s][:], in0=xpool[s][:],
                                                                 scalar=rs[:, hs:hs + 1], in1=fnw[:],
                                                                 op0=ALU.mult, op1=ALU.mult),
                    (xr, rsr, consts_r), (xr,))
            pg.dma("pool", xo_sem[s], Xdst(l)[n * 128:(n + 1) * 128, :], xpool[s][:],
                   reads=(xr,), writes=(Xres(l + 1, n),))

    for l in range(depth):
        layer_setup(l)
        norm_a(l, 0)
        norm_b(l, 0)
        gates(l, 0)
        for g in range(NG):
            group(l, g)
            if debug and l == 0 and g == 0:
                pg.dma("sp", setup_sem, dbg, arena[:, 0:4096],
                       reads=tuple(res(("omix", c)) for c in range(16)), writes=())
            wout_stage(l, g)
    pg.finish("sp")
    pg.finish("pool")
    pg.finish("act")
    pg.emit()
    stack.close()
    return nc


def _consts():
    bf = ml_dtypes.bfloat16
    ident = np.eye(128, dtype=np.float32).astype(bf)
    p = np.arange(128)[:, None]
    c = np.arange(128)[None, :]
    mask = (p <= c).astype(np.float32).astype(bf)
    scan = np.ones((128, G), dtype=np.float32)
    scan[:, ::128] = 0.0
    scan = scan.astype(bf)
    ones = np.full((128, 128), 1.0 / 512.0, dtype=np.float32).astype(bf)
    return {"c_ident": ident, "c_mask": mask, "c_scan": scan, "c_ones": ones}


_NC_CACHE = {}


DBG_OUT = []


def run(inputs, T, depth, n_cores, debug=False):
    key = (T, depth)
    if key not in _NC_CACHE:
        _NC_CACHE[key] = build_program(T, depth, debug)
    nc = _NC_CACHE[key]
    cst = _consts()
    shared = {}
    for k in ("norm_w", "w_in", "gla_w_gate_up", "gla_b_gate", "gla_norm_w", "sc_conv_w", "cf_conv_w",
              "cf_conv_b", "cf_ln_w", "cf_ln_b", "w_out"):
        shared[k] = np.ascontiguousarray(np.asarray(inputs[k], dtype=np.float32))
    shared["final_norm_w"] = np.ascontiguousarray(np.asarray(inputs["final_norm_w"], dtype=np.float32)).reshape(1, D)
    shared.update(cst)
    x = np.asarray(inputs["x"], dtype=np.float32)
    in_maps = []
    for c in range(n_cores):
        m = dict(shared)
        m["x"] = np.ascontiguousarray(x[c])
        in_maps.append(m)
    res = run_bass_kernel_spmd(nc, in_maps, core_ids=list(range(n_cores)))
    if debug:
        DBG_OUT[:] = [np.asarray(r["dbg"]) for r in res.results]
    return np.stack([np.asarray(r["out"], dtype=np.float32) for r in res.results], axis=0)


def kernel(**inputs):
    return run(inputs, SEQ, DEPTH, NCORES)
```

```python
import math
from contextlib import ExitStack

import numpy as np
import ml_dtypes

import concourse.bass as bass
import concourse.mybir as mybir
from concourse.bass_utils import run_bass_kernel_spmd

F32 = mybir.dt.float32
BF16 = mybir.dt.bfloat16
AF = mybir.ActivationFunctionType
ALU = mybir.AluOpType

D = 1024
KC = 8
DIN = 6672
DMIX = 2048
NH = 4
DEPTH = 4
SEQ = 8192
NCORES = 8
TPG = 2
G = TPG * 128
NORM_EPS = 1e-6
LN_EPS = 1e-5
NU = 17
RING = 4

UCOL = {"q": 0, "k": 512, "v0": 1024, "v1": 1536, "gg0": 2048, "gg1": 2560,
        "sb": 3088, "sc": 3600, "sh": 4112, "sgate": 4624, "ca": 5136, "cb": 5648, "cgate": 6160}
LR_COL = 3072
UORDER = ["cb", "ca", "sc", "sh", "sgate", "sb", "cgate", "v0", "v1", "gg0", "gg1", "q", "k",
          "wo2", "wo3", "wo0", "wo1"]
UIDX = {n: i for i, n in enumerate(UORDER)}


class Res:
    __slots__ = ("w", "r")

    def __init__(self):
        self.w = None
        self.r = {}


class DSem:
    def __init__(self, sem):
        self.sem = sem
        self.cnt = 0


class Eng:
    def __init__(self, name, sem):
        self.name = name
        self.sem = sem
        self.cnt = 0
        self.seen = {}
        self.prog = []


class Prog:
    def __init__(self, nc, stack):
        self.nc = nc
        self.stack = stack
        self.E = {}
        for n in ("pe", "act", "dve", "pool", "sp"):
            self.E[n] = Eng(n, stack.enter_context(nc.semaphore("tl_" + n)))
        self.dsems = []

    def dsem(self, name):
        d = DSem(self.stack.enter_context(self.nc.semaphore(name)))
        self.dsems.append(d)
        return d

    def _wait(self, e, tok):
        sem, val, owner = tok
        k = id(sem)
        if e.seen.get(k, 0) >= val:
            return
        e.seen[k] = val
        e.prog.append(("w", sem, val))

    def _deps(self, e, reads, writes):
        for r in reads:
            if r.w is not None:
                t = r.w
                if not (t[2] is e and e.name == "pe"):
                    self._wait(e, t)
        for w in writes:
            if w.w is not None and w.w[2] is not e:
                self._wait(e, w.w)
            for t in w.r.values():
                if t[2] is not e:
                    self._wait(e, t)

    def _mark(self, tok, reads, writes):
        k = id(tok[0])
        for r in reads:
            r.r[k] = tok
        for w in writes:
            w.w = tok
            w.r = {}

    def op(self, en, fn, reads=(), writes=()):
        e = self.E[en]
        self._deps(e, reads, writes)
        e.cnt += 1
        tok = (e.sem, e.cnt, e)
        e.prog.append(("o", fn))
        self._mark(tok, reads, writes)
        return tok

    def dma(self, en, ds, out, in_, reads=(), writes=(), slow=False):
        e = self.E[en]
        self._deps(e, reads, writes)
        ds.cnt += 1
        tok = (ds.sem, 16 * ds.cnt, None)
        e.prog.append(("d", out, in_, ds.sem, slow))
        self._mark(tok, reads, writes)
        return tok

    def barrier(self):
        for e in self.E.values():
            for o in self.E.values():
                if o is not e and o.cnt > 0:
                    self._wait(e, (o.sem, o.cnt, o))
            for d in self.dsems:
                if d.cnt > 0:
                    self._wait(e, (d.sem, 16 * d.cnt, None))

    def finish(self, en):
        e = self.E[en]
        for d in self.dsems:
            if d.cnt > 0:
                self._wait(e, (d.sem, 16 * d.cnt, None))

    def emit(self):
        nc = self.nc

        def run(e, eng):
            for it in e.prog:
                if it[0] == "w":
                    eng.wait_ge(it[1], it[2])
                elif it[0] == "o":
                    it[1](eng).then_inc(e.sem, 1)
                else:
                    if it[4]:
                        eng.dma_start(out=it[1], in_=it[2], allow_slow_non_contiguous=True).then_inc(it[3], 16)
                    else:
                        eng.dma_start(out=it[1], in_=it[2]).then_inc(it[3], 16)

        with nc.Block() as blk:
            @blk.tensor
            def _(eng):
                run(self.E["pe"], eng)

            @blk.scalar
            def _(eng):
                run(self.E["act"], eng)

            @blk.vector
            def _(eng):
                run(self.E["dve"], eng)

            @blk.gpsimd
            def _(eng):
                run(self.E["pool"], eng)

            @blk.sync
            def _(eng):
                run(self.E["sp"], eng)


def build_program(T=SEQ, depth=DEPTH, debug=False):
    NT = T // 128
    NG = T // G
    nc = bass.Bass("TRN2", target_bir_lowering=False)
    stack = ExitStack()
    pg = Prog(nc, stack)

    def dram(name, shape, dt, kind):
        return nc.dram_tensor(name, list(shape), dt, kind=kind).ap()

    x_in = dram("x", [T, D], F32, "ExternalInput")
    norm_w = dram("norm_w", [depth, D], F32, "ExternalInput")
    w_in = dram("w_in", [depth, D, DIN], F32, "ExternalInput")
    w_gup = dram("gla_w_gate_up", [depth, 16, 512], F32, "ExternalInput")
    b_gate = dram("gla_b_gate", [depth, 512], F32, "ExternalInput")
    gnorm_w = dram("gla_norm_w", [depth, 256], F32, "ExternalInput")
    sc_w = dram("sc_conv_w", [depth, 3, 512], F32, "ExternalInput")
    cf_w = dram("cf_conv_w", [depth, 31, 512], F32, "ExternalInput")
    cf_b = dram("cf_conv_b", [depth, 512], F32, "ExternalInput")
    ln_w = dram("cf_ln_w", [depth, 512], F32, "ExternalInput")
    ln_b = dram("cf_ln_b", [depth, 512], F32, "ExternalInput")
    w_out = dram("w_out", [depth, DMIX, D], F32, "ExternalInput")
    fnorm_w = dram("final_norm_w", [1, D], F32, "ExternalInput")
    c_ident = dram("c_ident", [128, 128], BF16, "ExternalInput")
    c_mask = dram("c_mask", [128, 128], BF16, "ExternalInput")
    c_scan = dram("c_scan", [128, G], BF16, "ExternalInput")
    c_ones = dram("c_ones", [128, 128], BF16, "ExternalInput")
    out = dram("out", [T, D], F32, "ExternalOutput")
    dbg = dram("dbg", [128, 4096], BF16, "ExternalOutput") if debug else None
    xs = [dram("xs0", [T, D], F32, "Internal"), dram("xs1", [T, D], F32, "Internal")]
    wbf = dram("wbf", [depth * NU, 128, 4096], BF16, "Internal")

    def sb(name, shape, dt):
        return stack.enter_context(nc.sbuf_tensor(name, list(shape), dt))

    ident = sb("ident", [128, 128], BF16)
    maskt = sb("maskt", [128, 128], BF16)
    scanm = sb("scanm", [128, G], BF16)
    onesm = sb("onesm", [128, 128], BF16)
    fnw = sb("fnw", [128, D], F32)
    cst = sb("cst", [128, 4], F32)
    nwT = sb("nwT", [128, depth * 8], F32)
    gnwT = sb("gnwT", [128, depth * 2], F32)
    nbgT = sb("nbgT", [128, depth * 4], F32)
    cfbT = sb("cfbT", [128, depth * 4], F32)
    lnwT = sb("lnwT", [128, depth * 4], F32)
    lnbT = sb("lnbT", [128, depth * 4], F32)
    cfwT = sb("cfwT", [128, depth * 4, 31], F32)
    scwT = sb("scwT", [128, depth * 4, 3], F32)
    wlr = sb("wlr", [128, depth, 8, 16], BF16)
    wg = sb("wg", [16, 512], F32)
    dcf = sb("dcf", [128, 4, 31, 128], BF16)
    dsc = sb("dsc", [128, 4, 3, 128], BF16)
    ring = [sb(f"ring{i}", [128, 4096], BF16) for i in range(RING)]
    xpool = [sb(f"xp{i}", [128, D], F32) for i in range(4)]
    hbf = [sb(f"hbf{i}", [128, D], BF16) for i in range(2)]
    hT = sb("hT", [128, 8, G], BF16)
    NTMP = 6
    tmp = [sb(f"tmp{i}", [128, G], F32) for i in range(NTMP)]
    glu = [[sb(f"glu{p}{j}", [128, 30 + G], BF16) for j in range(4)] for p in range(2)]
    ubuf = [[sb(f"u{p}{j}", [128, 2 + G], BF16) for j in range(4)] for p in range(2)]
    cbf = [sb(f"cbf{j}", [128, G], BF16) for j in range(4)]
    csq = [sb(f"csq{j}", [128, G], BF16) for j in range(4)]
    mean_sb = sb("mean_sb", [128, G], F32)
    rstd_sb = sb("rstd_sb", [128, G], F32)
    t1 = [sb(f"t1{j}", [128, G], BF16) for j in range(4)]
    glr = sb("glr", [16, G], F32)
    dbuf = [sb(f"dbuf{h}", [128, G], F32) for h in range(4)]
    clast = sb("clast", [128, 4, TPG], F32)
    dec = sb("dec", [128, 4, TPG], F32)
    qi = [sb(f"qi{h}", [128, G], BF16) for h in range(4)]
    ki = [sb(f"ki{h}", [128, G], BF16) for h in range(4)]
    kiT = [sb(f"kiT{t}", [128, 512], BF16) for t in range(TPG)]
    arena = sb("arena", [128, 8192], BF16)
    omix = arena[:, 0:16 * G].rearrange("p (m c) -> p m c", c=G)
    v_sb = [arena[:, 4096 + t * 1024:4096 + (t + 1) * 1024] for t in range(TPG)]
    gsil = [arena[:, 6144 + t * 1024:6144 + (t + 1) * 1024] for t in range(TPG)]
    PTs = [sb(f"PTs{i}", [128, 4, 128], BF16) for i in range(2)]
    S = sb("S", [128, 4, 256], F32)
    Dbf = [sb(f"Dbf{i}", [128, 4, 256], BF16) for i in range(2)]
    on = [sb(f"on{i}", [128, D], BF16) for i in range(2)]
    junk = sb("junk", [128, D], BF16)
    ss = sb("ss", [128, 8], F32)
    rs = sb("rs", [128, 8], F32)
    ss4 = sb("ss4", [128, 8], F32)
    rs4 = sb("rs4", [128, 8], F32)
    stg = [sb(f"stg{i}", [128, 4096], F32) for i in range(2)]
    sbo = [arena[:, i * 4096:(i + 1) * 4096] for i in range(2)]

    ps = [stack.enter_context(nc.psum_tensor(f"ps{i}", [128, 512], F32)) for i in range(8)]
    ps_r = [Res() for _ in range(8)]
    NFB = 6
    fb = [0]
    tb = [0]

    def bank():
        b = fb[0] % NFB
        fb[0] += 1
        return b

    def tbank():
        b = NFB + (tb[0] % 2)
        tb[0] += 1
        return b

    R = {}

    def res(key):
        if key not in R:
            R[key] = Res()
        return R[key]

    setup_sem = pg.dsem("d_setup")
    stg_sem = [pg.dsem(f"d_stg{i}") for i in range(2)]
    sbo_sem = [pg.dsem(f"d_sbo{i}") for i in range(2)]
    ring_sem = [pg.dsem(f"d_ring{i}") for i in range(RING)]
    xp_sem = [pg.dsem(f"d_xp{i}") for i in range(4)]
    xo_sem = [pg.dsem(f"d_xo{i}") for i in range(4)]
    wg_sem = pg.dsem("d_wg")

    def act(fn, reads, writes):
        return pg.op("act", fn, reads, writes)

    def dve(fn, reads, writes):
        return pg.op("dve", fn, reads, writes)

    def pool(fn, reads, writes):
        return pg.op("pool", fn, reads, writes)

    def pe(fn, reads, writes):
        return pg.op("pe", fn, reads, writes)

    LNSCALE = float(-0.5 * math.log(128.0))
    consts_r = res("consts")

    def setup_dma(o, i, slow=False):
        pg.dma("sp", setup_sem, o, i, reads=(), writes=(consts_r,), slow=slow)

    setup_dma(ident[:], c_ident)
    setup_dma(maskt[:], c_mask)
    setup_dma(scanm[:], c_scan)
    setup_dma(onesm[:], c_ones)
    setup_dma(fnw[:], fnorm_w.partition_broadcast(128))
    setup_dma(nwT[:], norm_w.rearrange("l (k p) -> p (l k)", p=128), slow=True)
    setup_dma(gnwT[:], gnorm_w.rearrange("l (k p) -> p (l k)", p=128), slow=True)
    setup_dma(nbgT[:], b_gate.rearrange("l (k p) -> p (l k)", p=128), slow=True)
    setup_dma(cfbT[:], cf_b.rearrange("l (k p) -> p (l k)", p=128), slow=True)
    setup_dma(lnwT[:], ln_w.rearrange("l (k p) -> p (l k)", p=128), slow=True)
    setup_dma(lnbT[:], ln_b.rearrange("l (k p) -> p (l k)", p=128), slow=True)
    for l in range(depth):
        for j in range(4):
            setup_dma(cfwT[:, l * 4 + j, :], cf_w[l, :, j * 128:(j + 1) * 128].rearrange("k p -> p k"), slow=True)
            setup_dma(scwT[:, l * 4 + j, :], sc_w[l, :, j * 128:(j + 1) * 128].rearrange("k p -> p k"), slow=True)
    pool(lambda e: e.memset(cst[:, 0:1], float(NORM_EPS)), (), (consts_r,))
    pool(lambda e: e.memset(cst[:, 1:2], float(LN_EPS)), (), (consts_r,))
    pool(lambda e: e.memset(cst[:, 2:3], LNSCALE), (), (consts_r,))
    pool(lambda e: e.memset(cst[:, 3:4], 1.0), (), (consts_r,))
    pg.barrier()
    dve(lambda e: e.tensor_scalar(out=nbgT[:], in0=nbgT[:], scalar1=-1.0, scalar2=None, op0=ALU.mult),
        (consts_r,), (consts_r,))

    cvt_engs = ["act", "dve"]
    cvt_i = [0]

    def convert(si, views_in, views_out, scalars):
        for vin, vout, sc in zip(views_in, views_out, scalars):
            en = cvt_engs[cvt_i[0] % 2]
            cvt_i[0] += 1
            rd = (res(("stg", si)), consts_r)
            wr = (res(("sbo", si)),)
            if sc is None:
                if en == "act":
                    act(lambda e, a=vin, b=vout: e.copy(out=b, in_=a), rd, wr)
                else:
                    pg.op(en, lambda e, a=vin, b=vout: e.tensor_copy(out=b, in_=a), rd, wr)
            else:
                if en == "act":
                    act(lambda e, a=vin, b=vout, s=sc: e.activation(out=b, in_=a, func=AF.Copy, scale=s), rd, wr)
                else:
                    pg.op(en, lambda e, a=vin, b=vout, s=sc: e.tensor_scalar(
                        out=b, in0=a, scalar1=s, scalar2=None, op0=ALU.mult), rd, wr)

    cv = [0]
    for l in range(depth):
        for u, name in enumerate(UORDER):
            si = cv[0] % 2
            cv[0] += 1
            if name.startswith("wo"):
                i = int(name[2])
                src = w_out[l, i * 512:(i + 1) * 512, :].rearrange("(m p) c -> p m c", p=128)
                dst = stg[si][:].rearrange("p (m c) -> p m c", c=1024)
            else:
                c0 = UCOL[name]
                src = w_in[l, :, c0:c0 + 512].rearrange("(k p) c -> p k c", p=128)
                dst = stg[si][:].rearrange("p (k c) -> p k c", c=512)
            pg.dma("sp", stg_sem[si], dst, src, reads=(), writes=(res(("stg", si)),))
            vin, vout, scs = [], [], []
            if name.startswith("wo"):
                i = int(name[2])
                for m in range(4):
                    for hf in range(2):
                        a = m * 1024 + hf * 512
                        vin.append(stg[si][:, a:a + 512])
                        vout.append(sbo[si][:, a:a + 512])
                        scs.append(gnwT[:, l * 2 + (m % 2):l * 2 + (m % 2) + 1] if i < 2 else None)
            else:
                for k in range(8):
                    vin.append(stg[si][:, k * 512:(k + 1) * 512])
                    vout.append(sbo[si][:, k * 512:(k + 1) * 512])
                    scs.append(nwT[:, l * 8 + k:l * 8 + k + 1])
            convert(si, vin, vout, scs)
            pg.dma("pool", sbo_sem[si], wbf[l * NU + u], sbo[si], reads=(res(("sbo", si)),),
                   writes=(res(("wbf", l, u)),))
        si = cv[0] % 2
        cv[0] += 1
        src = w_in[l, :, LR_COL:LR_COL + 16].rearrange("(k p) c -> p k c", p=128)
        dst = stg[si][:, 0:128].rearrange("p (k c) -> p k c", c=16)
        pg.dma("sp", stg_sem[si], dst, src, reads=(), writes=(res(("stg", si)),), slow=True)
        for k in range(8):
            dve(lambda e, a=stg[si][:, k * 16:(k + 1) * 16], b=wlr[:, l, k, :],
                s=nwT[:, l * 8 + k:l * 8 + k + 1]: e.tensor_scalar(out=b, in0=a, scalar1=s, scalar2=None,
                                                                   op0=ALU.mult),
                (res(("stg", si)), consts_r), (res("wlr"),))
    pg.barrier()

    useq = []
    for l in range(depth):
        for g in range(NG):
            for name in UORDER:
                useq.append((l, g, name))
    loaded = [0]

    def ensure_loaded(upto):
        while loaded[0] <= upto and loaded[0] < len(useq):
            i = loaded[0]
            l, g, name = useq[i]
            s = i % RING
            pg.dma("sp", ring_sem[s], ring[s][:], wbf[l * NU + UIDX[name]],
                   reads=(res(("wbf", l, UIDX[name])),), writes=(res(("ring", s)),))
            loaded[0] += 1

    upos = [0]

    def next_unit(l, g, name, hold=0):
        i = upos[0]
        assert useq[i] == (l, g, name), (useq[i], l, g, name)
        upos[0] += 1
        ensure_loaded(i + RING - 1 - hold)
        s = i % RING
        return ring[s], res(("ring", s))

    tmp_i = [0]

    def gettmp():
        i = tmp_i[0] % NTMP
        tmp_i[0] += 1
        return tmp[i], res(("tmp", i))

    hT_r = [res(("hT", t)) for t in range(TPG)]

    def Xsrc(l):
        return x_in if l == 0 else xs[(l - 1) % 2]

    def Xdst(l):
        return out if l == depth - 1 else xs[l % 2]

    def Xres(l, n):
        return res(("X", l, n))

    def layer_setup(l):
        pg.dma("sp", wg_sem, wg[:], w_gup[l], reads=(), writes=(res("wg"),))
        for j in range(4):
            for k in range(31):
                en = "dve"
                pg.op(en, lambda e, o=dcf[:, j, k, :], s=cfwT[:, l * 4 + j, k:k + 1]: e.tensor_scalar(
                    out=o, in0=ident[:], scalar1=s, scalar2=None, op0=ALU.mult),
                    (consts_r,), (res("dcf"),))
            for k in range(3):
                dve(lambda e, o=dsc[:, j, k, :], s=scwT[:, l * 4 + j, k:k + 1]: e.tensor_scalar(
                    out=o, in0=ident[:], scalar1=s, scalar2=None, op0=ALU.mult),
                    (consts_r,), (res("dsc"),))
        pool(lambda e: e.memset(S[:], 0.0), (), (res("S"),))

    xslot = [0]

    def norm_a(l, g):
        for tt in range(TPG):
            n = g * TPG + tt
            s = (g % 2) * TPG + tt
            hs = n % 2
            xr = res(("xp", s))
            pg.dma("sp", xp_sem[s], xpool[s][:], Xsrc(l)[n * 128:(n + 1) * 128, :],
                   reads=(Xres(l, n),), writes=(xr,))
            ssr = res(("ss", hs))
            act(lambda e, s=s, hs=hs: e.activation(out=junk[:], in_=xpool[s][:], func=AF.Square,
                                                   accum_out=ss[:, hs:hs + 1]),
                (xr,), (ssr,))
            rsr = res(("rs", hs))
            act(lambda e, hs=hs: e.activation(out=rs[:, hs:hs + 1], in_=ss[:, hs:hs + 1], func=AF.Ln,
                                              bias=cst[:, 0:1], scale=1.0 / D),
                (ssr, consts_r), (rsr,))
            act(lambda e, hs=hs: e.activation(out=rs[:, hs:hs + 1], in_=rs[:, hs:hs + 1], func=AF.Exp,
                                              scale=-0.5),
                (rsr,), (rsr,))
            hr = res(("hbf", hs))
            act(lambda e, s=s, hs=hs: e.activation(out=hbf[hs][:], in_=xpool[s][:], func=AF.Copy,
                                                   scale=rs[:, hs:hs + 1]),
                (xr, rsr), (hr,))

    def norm_b(l, g):
        for tt in range(TPG):
            n = g * TPG + tt
            hs = n % 2
            hr = res(("hbf", hs))
            b = tbank()
            pv = ps[b].bitcast(BF16)

            def tr(e, hs=hs, pv=pv):
                ins = None
                for k in range(8):
                    ins = e.transpose(out=pv[:, k * 128:(k + 1) * 128], in_=hbf[hs][:, k * 128:(k + 1) * 128],
                                      identity=ident[:])
                return ins
            pe(tr, (hr, consts_r), (ps_r[b],))
            act(lambda e, pv=pv, tt=tt: e.copy(out=hT[:, :, tt * 128:(tt + 1) * 128],
                                               in_=pv[:, 0:1024].rearrange("p (k c) -> p k c", c=128)),
                (ps_r[b],), (hT_r[tt],))

    def fm_chunk(W, Wr, j, M=128):
        b = bank()

        def mm(e, b=b, j=j):
            ins = None
            for k in range(8):
                ins = e.matmul(ps[b][0:M, 0:G], lhsT=W[:, k * 512 + j * 128:k * 512 + j * 128 + M],
                               rhs=hT[:, k, :], start=(k == 0), stop=(k == 7))
            return ins
        pe(mm, (Wr,) + tuple(hT_r), (ps_r[b],))
        return b

    def tm_tile(W, Wr, tt):
        b = bank()

        def mm(e, b=b, tt=tt):
            ins = None
            for k in range(8):
                ins = e.matmul(ps[b][:, :], lhsT=hT[:, k, tt * 128:(tt + 1) * 128],
                               rhs=W[:, k * 512:(k + 1) * 512], start=(k == 0), stop=(k == 7))
            return ins
        pe(mm, (Wr, hT_r[tt]), (ps_r[b],))
        return b


    def gates(l, g):
        b = bank()

        def mm_lr(e, b=b):
            ins = None
            for k in range(8):
                ins = e.matmul(ps[b][0:16, 0:G], lhsT=wlr[:, l, k, :], rhs=hT[:, k, :],
                               start=(k == 0), stop=(k == 7))
            return ins
        pe(mm_lr, (res("wlr"),) + tuple(hT_r), (ps_r[b],))
        glr_r = res("glr")
        act(lambda e, b=b: e.copy(out=glr[:], in_=ps[b][0:16, 0:G]), (ps_r[b],), (glr_r,))
        for h in range(4):
            b = bank()
            pe(lambda e, b=b, h=h: e.matmul(ps[b][:, 0:G], lhsT=wg[:, h * 128:(h + 1) * 128], rhs=glr[:],
                                            start=True, stop=True),
               (res("wg"), glr_r), (ps_r[b],))
            t_e, t_er = gettmp()
            act(lambda e, b=b, h=h, t=t_e: e.activation(out=t[:], in_=ps[b][:, 0:G], func=AF.Exp,
                                                        bias=nbgT[:, l * 4 + h:l * 4 + h + 1], scale=-1.0),
                (ps_r[b], consts_r), (t_er,))
            act(lambda e, t=t_e: e.activation(out=t[:], in_=t[:], func=AF.Ln, bias=cst[:, 3:4], scale=1.0),
                (t_er,), (t_er,))
            dr = res(("d", h))
            dve(lambda e, t=t_e, h=h: e.tensor_tensor_scan(out=dbuf[h][:], data0=scanm[:], data1=t[:],
                                                           initial=0.0, op0=ALU.mult, op1=ALU.add),
                (t_er, consts_r), (dr,))
            clr = res(("cl", h))
            dve(lambda e, h=h: e.tensor_copy(
                out=clast[:, h, :], in_=dbuf[h][:].rearrange("p (t c) -> p t c", c=128)[:, :, 127]),
                (dr,), (clr,))
            dve(lambda e, h=h: e.tensor_tensor(
                out=dbuf[h][:].rearrange("p (t c) -> p t c", c=128),
                in0=dbuf[h][:].rearrange("p (t c) -> p t c", c=128),
                in1=clast[:, h, :].unsqueeze(2).to_broadcast([128, TPG, 128]), op=ALU.subtract),
                (dr, clr), (dr,))
            act(lambda e, h=h: e.activation(out=dec[:, h, :], in_=clast[:, h, :], func=AF.Exp,
                                            scale=-1.0 / 16.0),
                (clr,), (res(("dec", h)),))


    def group(l, g):
        par = g % 2
        last_layer = (l == depth - 1)
        Wb, Wbr = next_unit(l, g, "cb")
        Wa, War = next_unit(l, g, "ca", hold=1)
        for j in range(4):
            gr = res(("glu", par, j))
            if g == 0:
                pool(lambda e, j=j: e.memset(glu[par][j][:, 0:30], 0.0), (), (gr,))
            else:
                pool(lambda e, j=j: e.tensor_copy(out=glu[par][j][:, 0:30], in_=glu[1 - par][j][:, G:G + 30]),
                     (res(("glu", 1 - par, j)),), (gr,))
            bb = fm_chunk(Wb, Wbr, j)
            t_s, t_sr = gettmp()
            act(lambda e, b=bb, t=t_s: e.activation(out=t[:], in_=ps[b][:, 0:G], func=AF.Sigmoid),
                (ps_r[bb],), (t_sr,))
            ba = fm_chunk(Wa, War, j)
            dve(lambda e, b=ba, t=t_s, j=j: e.tensor_tensor(out=glu[par][j][:, 30:30 + G], in0=ps[b][:, 0:G],
                                                            in1=t[:], op=ALU.mult),
                (ps_r[ba], t_sr), (gr,))
        Wc, Wcr = next_unit(l, g, "sc")
        Wh, Whr = next_unit(l, g, "sh", hold=1)
        for j in range(4):
            ur = res(("u", par, j))
            if g == 0:
                pool(lambda e, j=j: e.memset(ubuf[par][j][:, 0:2], 0.0), (), (ur,))
            else:
                pool(lambda e, j=j: e.tensor_copy(out=ubuf[par][j][:, 0:2], in_=ubuf[1 - par][j][:, G:G + 2]),
                     (res(("u", 1 - par, j)),), (ur,))
            bc = fm_chunk(Wc, Wcr, j)
            t_s, t_sr = gettmp()
            dve(lambda e, b=bc, t=t_s: e.tensor_copy(out=t[:], in_=ps[b][:, 0:G]), (ps_r[bc],), (t_sr,))
            bh = fm_chunk(Wh, Whr, j)
            dve(lambda e, b=bh, t=t_s, j=j: e.tensor_tensor(out=ubuf[par][j][:, 2:2 + G], in0=ps[b][:, 0:G],
                                                            in1=t[:], op=ALU.mult),
                (ps_r[bh], t_sr), (ur,))
        for j in range(4):
            b = bank()

            def mmc(e, b=b, j=j):
                ins = None
                for k in range(31):
                    ins = e.matmul(ps[b][:, 0:G], lhsT=dcf[:, j, k, :], rhs=glu[par][j][:, k:k + G],
                                   start=(k == 0), stop=(k == 30))
                return ins
            pe(mmc, (res("dcf"), res(("glu", par, j))), (ps_r[b],))
            act(lambda e, b=b, j=j: e.activation(out=cbf[j][:], in_=ps[b][:, 0:G], func=AF.Identity,
                                                 bias=cfbT[:, l * 4 + j:l * 4 + j + 1], scale=1.0),
                (ps_r[b], consts_r), (res(("cbf", j)),))
            act(lambda e, b=b, j=j: e.activation(out=csq[j][:], in_=ps[b][:, 0:G], func=AF.Square,
                                                 bias=cfbT[:, l * 4 + j:l * 4 + j + 1], scale=1.0),
                (ps_r[b], consts_r), (res(("csq", j)),))
        Wsg, Wsgr = next_unit(l, g, "sgate")
        Wsb, Wsbr = next_unit(l, g, "sb", hold=1)
        for j in range(4):
            bg_ = fm_chunk(Wsg, Wsgr, j)
            t_s, t_sr = gettmp()
            act(lambda e, b=bg_, t=t_s: e.activation(out=t[:], in_=ps[b][:, 0:G], func=AF.Silu),
                (ps_r[bg_],), (t_sr,))
            bs = fm_chunk(Wsb, Wsbr, j)
            dve(lambda e, b=bs, t=t_s, j=j: e.tensor_tensor(out=t1[j][:], in0=ps[b][:, 0:G], in1=t[:],
                                                            op=ALU.mult),
                (ps_r[bs], t_sr), (res(("t1", j)),))
        for j in range(4):
            b = bank()

            def mms(e, b=b, j=j):
                ins = None
                for k in range(3):
                    ins = e.matmul(ps[b][:, 0:G], lhsT=dsc[:, j, k, :], rhs=ubuf[par][j][:, k:k + G],
                                   start=(k == 0), stop=(k == 2))
                return ins
            pe(mms, (res("dsc"), res(("u", par, j))), (ps_r[b],))
            dve(lambda e, b=b, j=j: e.tensor_tensor(out=omix[:, 8 + j, :], in0=ps[b][:, 0:G], in1=t1[j][:],
                                                    op=ALU.mult),
                (ps_r[b], res(("t1", j))), (res(("omix", 8 + j)),))
        b1 = bank()

        def mm_mean(e, b=b1):
            ins = None
            for j in range(4):
                ins = e.matmul(ps[b][:, 0:G], lhsT=onesm[:], rhs=cbf[j][:], start=(j == 0), stop=(j == 3))
            return ins
        pe(mm_mean, tuple(res(("cbf", j)) for j in range(4)) + (consts_r,), (ps_r[b1],))
        b2 = bank()

        def mm_msq(e, b=b2):
            ins = None
            for j in range(4):
                ins = e.matmul(ps[b][:, 0:G], lhsT=onesm[:], rhs=csq[j][:], start=(j == 0), stop=(j == 3))
            return ins
        pe(mm_msq, tuple(res(("csq", j)) for j in range(4)) + (consts_r,), (ps_r[b2],))
        mr = res("mean")
        rr = res("rstd")
        act(lambda e, b=b1: e.copy(out=mean_sb[:], in_=ps[b][:, 0:G]), (ps_r[b1],), (mr,))
        t_m, t_mr = gettmp()
        dve(lambda e, t=t_m: e.tensor_tensor(out=t[:], in0=mean_sb[:], in1=mean_sb[:], op=ALU.mult),
            (mr,), (t_mr,))
        dve(lambda e, t=t_m, b=b2: e.tensor_tensor(out=t[:], in0=ps[b][:, 0:G], in1=t[:], op=ALU.subtract),
            (ps_r[b2], t_mr), (t_mr,))
        dve(lambda e, t=t_m: e.tensor_scalar(out=t[:], in0=t[:], scalar1=0.0, scalar2=None, op0=ALU.max),
            (t_mr,), (t_mr,))
        act(lambda e, t=t_m: e.activation(out=t[:], in_=t[:], func=AF.Ln, bias=cst[:, 1:2], scale=1.0),
            (t_mr, consts_r), (t_mr,))
        act(lambda e, t=t_m: e.activation(out=rstd_sb[:], in_=t[:], func=AF.Exp, scale=-0.5),
            (t_mr,), (rr,))
        for j in range(4):
            t_y, t_yr = gettmp()
            pool(lambda e, t=t_y, j=j: e.tensor_tensor(out=t[:], in0=cbf[j][:], in1=mean_sb[:],
                                                       op=ALU.subtract),
                 (res(("cbf", j)), mr), (t_yr,))
            pool(lambda e, t=t_y: e.tensor_tensor(out=t[:], in0=t[:], in1=rstd_sb[:], op=ALU.mult),
                 (t_yr, rr), (t_yr,))
            act(lambda e, t=t_y, j=j: e.activation(out=omix[:, 12 + j, :], in_=t[:], func=AF.Silu,
                                                   bias=lnbT[:, l * 4 + j:l * 4 + j + 1],
                                                   scale=lnwT[:, l * 4 + j:l * 4 + j + 1]),
                (t_yr, consts_r), (res(("omix", 12 + j)),))
        W, Wr = next_unit(l, g, "cgate")
        for j in range(4):
            b = fm_chunk(W, Wr, j)
            t_s, t_sr = gettmp()
            act(lambda e, b=b, t=t_s: e.activation(out=t[:], in_=ps[b][:, 0:G], func=AF.Silu),
                (ps_r[b],), (t_sr,))
            pool(lambda e, t=t_s, j=j: e.tensor_tensor(out=omix[:, 12 + j, :], in0=omix[:, 12 + j, :], in1=t[:],
                                                       op=ALU.mult),
                 (t_sr, res(("omix", 12 + j))), (res(("omix", 12 + j)),))
        if g + 1 < NG:
            norm_a(l, g + 1)
        for hf, name in enumerate(("v0", "v1")):
            W, Wr = next_unit(l, g, name)
            for tt in range(TPG):
                b = tm_tile(W, Wr, tt)
                dve(lambda e, b=b, tt=tt, hf=hf: e.tensor_copy(out=v_sb[tt][:, hf * 512:(hf + 1) * 512],
                                                               in_=ps[b][:, :]),
                    (ps_r[b],), (res(("v", tt)),))
        for hf, name in enumerate(("gg0", "gg1")):
            W, Wr = next_unit(l, g, name)
            for tt in range(TPG):
                b = tm_tile(W, Wr, tt)
                act(lambda e, b=b, tt=tt, hf=hf: e.activation(out=gsil[tt][:, hf * 512:(hf + 1) * 512],
                                                              in_=ps[b][:, :], func=AF.Silu),
                    (ps_r[b],), (res(("gsil", tt)),))
        for name, dst, sc_, bi_ in (("q", qi, -1.0 / 16.0, cst[:, 2:3]), ("k", ki, 1.0 / 16.0, 0.0)):
            W, Wr = next_unit(l, g, name)
            facs = []
            for h in range(4):
                t_f, t_fr = gettmp()
                act(lambda e, t=t_f, h=h, sc_=sc_, bi_=bi_: e.activation(out=t[:], in_=dbuf[h][:], func=AF.Exp,
                                                                         bias=bi_, scale=sc_),
                    (res(("d", h)),), (t_fr,))
                facs.append((t_f, t_fr))
            for h in range(4):
                b = fm_chunk(W, Wr, h)
                t_f, t_fr = facs[h]
                dve(lambda e, b=b, t=t_f, h=h, dst=dst: e.tensor_tensor(out=dst[h][:], in0=ps[b][:, 0:G],
                                                                        in1=t[:], op=ALU.mult),
                    (ps_r[b], t_fr), (res((name + "i", h)),))
        if g + 1 < NG:
            norm_b(l, g + 1)
        def tinfo(tt):
            n = g * TPG + tt
            return n % 2, slice(tt * 128, (tt + 1) * 128)

        def e_kiT(tt):
            b = tbank()
            pv = ps[b].bitcast(BF16)
            x2, tsl = tinfo(tt)

            def trk(e, pv=pv, tsl=tsl):
                ins = None
                for h in range(4):
                    ins = e.transpose(out=pv[:, h * 128:(h + 1) * 128], in_=ki[h][:, tsl], identity=ident[:])
                return ins
            pe(trk, tuple(res(("ki", h)) for h in range(4)) + (consts_r,), (ps_r[b],))
            dve(lambda e, pv=pv, tt=tt: e.tensor_copy(out=kiT[tt][:], in_=pv[:, 0:512]), (ps_r[b],),
                (res(("kiT", tt)),))

        def e_PT(tt):
            x2, tsl = tinfo(tt)
            b = bank()

            def mmp(e, b=b, tsl=tsl):
                ins = None
                for h in range(4):
                    ins = e.matmul(ps[b][:, h * 128:(h + 1) * 128], lhsT=ki[h][:, tsl], rhs=qi[h][:, tsl],
                                   start=True, stop=True)
                return ins
            pe(mmp, tuple(res(("ki", h)) for h in range(4)) + tuple(res(("qi", h)) for h in range(4)),
               (ps_r[b],))
            dve(lambda e, b=b, x2=x2: e.tensor_tensor(
                out=PTs[x2][:], in0=ps[b][:, :].rearrange("p (h c) -> p h c", c=128),
                in1=maskt[:].unsqueeze(1).to_broadcast([128, 4, 128]), op=ALU.mult),
                (ps_r[b], consts_r), (res(("PTs", x2)),))

        def e_state(tt):
            x2, tsl = tinfo(tt)
            dbr = res(("Dbf", x2))
            for h in range(4):
                act(lambda e, h=h, x2=x2, tt=tt: e.activation(out=Dbf[x2][:, h, :], in_=S[:, h, :], func=AF.Copy,
                                                              scale=dec[:, h, tt:tt + 1]),
                    (res("S"), res(("dec", h))), (dbr,))
            bkv = [bank(), bank()]
            for hp in range(2):
                def mmkv(e, b=bkv[hp], hp=hp, tt=tt):
                    ins = None
                    for hh in range(2):
                        h = hp * 2 + hh
                        ins = e.matmul(ps[b][:, hh * 256:(hh + 1) * 256], lhsT=kiT[tt][:, h * 128:(h + 1) * 128],
                                       rhs=v_sb[tt][:, h * 256:(h + 1) * 256], start=True, stop=True)
                    return ins
                pe(mmkv, (res(("kiT", tt)), res(("v", tt))), (ps_r[bkv[hp]],))
            for h in range(4):
                b = bkv[h // 2]
                dve(lambda e, b=b, h=h, tt=tt: e.scalar_tensor_tensor(
                    out=S[:, h, :], in0=S[:, h, :], scalar=dec[:, h, tt:tt + 1],
                    in1=ps[b][:, (h % 2) * 256:(h % 2 + 1) * 256], op0=ALU.mult, op1=ALU.add),
                    (res("S"), res(("dec", h)), ps_r[b]), (res("S"),))

        def e_o(tt):
            x2, tsl = tinfo(tt)
            bo = [bank(), bank()]
            for hp in range(2):
                def mmo(e, b=bo[hp], hp=hp, tt=tt, x2=x2, tsl=tsl):
                    ins = None
                    for hh in range(2):
                        h = hp * 2 + hh
                        e.matmul(ps[b][:, hh * 256:(hh + 1) * 256], lhsT=PTs[x2][:, h, :],
                                 rhs=v_sb[tt][:, h * 256:(h + 1) * 256], start=True, stop=False)
                        ins = e.matmul(ps[b][:, hh * 256:(hh + 1) * 256], lhsT=qi[h][:, tsl],
                                       rhs=Dbf[x2][:, h, :], start=False, stop=True)
                    return ins
                pe(mmo, (res(("PTs", x2)), res(("v", tt)), res(("Dbf", x2))) +
                   tuple(res(("qi", h)) for h in range(4)), (ps_r[bo[hp]],))
            s4r = res(("ss4", x2))
            for h in range(4):
                b = bo[h // 2]
                act(lambda e, b=b, h=h, x2=x2: e.activation(
                    out=junk[:, 0:256], in_=ps[b][:, (h % 2) * 256:(h % 2 + 1) * 256], func=AF.Square,
                    accum_out=ss4[:, x2 * 4 + h:x2 * 4 + h + 1]),
                    (ps_r[b],), (s4r,))
            r4r = res(("rs4", x2))
            act(lambda e, x2=x2: e.activation(out=rs4[:, x2 * 4:x2 * 4 + 4], in_=ss4[:, x2 * 4:x2 * 4 + 4],
                                              func=AF.Ln, bias=cst[:, 0:1], scale=1.0 / 256.0),
                (s4r, consts_r), (r4r,))
            act(lambda e, x2=x2: e.activation(out=rs4[:, x2 * 4:x2 * 4 + 4], in_=rs4[:, x2 * 4:x2 * 4 + 4],
                                              func=AF.Exp, scale=-0.5),
                (r4r,), (r4r,))
            onr = res(("on", x2))
            for h in range(4):
                b = bo[h // 2]
                dve(lambda e, b=b, h=h, x2=x2, tt=tt: e.scalar_tensor_tensor(
                    out=on[x2][:, h * 256:(h + 1) * 256], in0=ps[b][:, (h % 2) * 256:(h % 2 + 1) * 256],
                    scalar=rs4[:, x2 * 4 + h:x2 * 4 + h + 1], in1=gsil[tt][:, h * 256:(h + 1) * 256],
                    op0=ALU.mult, op1=ALU.mult),
                    (ps_r[b], r4r, res(("gsil", tt))), (onr,))

        def e_oaT(tt):
            x2, tsl = tinfo(tt)
            b = tbank()
            pv = ps[b].bitcast(BF16)

            def tro(e, pv=pv, x2=x2):
                ins = None
                for c in range(8):
                    ins = e.transpose(out=pv[:, c * 128:(c + 1) * 128], in_=on[x2][:, c * 128:(c + 1) * 128],
                                      identity=ident[:])
                return ins
            pe(tro, (res(("on", x2)), consts_r), (ps_r[b],))
            act(lambda e, pv=pv, tsl=tsl: e.copy(out=omix[:, 0:8, tsl],
                                                 in_=pv[:, 0:1024].rearrange("p (k c) -> p k c", c=128)),
                (ps_r[b],), tuple(res(("omix", c)) for c in range(8)))

        for tt in range(TPG):
            e_kiT(tt)
        for tt in range(TPG):
            e_PT(tt)
        for tt in range(TPG):
            e_state(tt)
        for tt in range(TPG):
            e_o(tt)
        if g + 1 < NG:
            gates(l, g + 1)
        for tt in range(TPG):
            e_oaT(tt)
        return

    def wout_stage(l, g):
        last_layer = (l == depth - 1)
        yb = [[bank(), bank()] for _ in range(TPG)]
        for io, i in enumerate((2, 3, 0, 1)):
            W, Wr = next_unit(l, g, f"wo{i}")
            for tt in range(TPG):
                for hf in range(2):
                    b = yb[tt][hf]

                    def mmy(e, b=b, i=i, io=io, tt=tt, hf=hf, W=W):
                        ins = None
                        for m in range(4):
                            mc = i * 4 + m
                            ins = e.matmul(ps[b][:, :], lhsT=omix[:, mc, tt * 128:(tt + 1) * 128],
                                           rhs=W[:, m * 1024 + hf * 512:m * 1024 + (hf + 1) * 512],
                                           start=(io == 0 and m == 0), stop=(io == 3 and m == 3))
                        return ins
                    pe(mmy, (Wr,) + tuple(res(("omix", i * 4 + m)) for m in range(4)), (ps_r[b],))
        for tt in range(TPG):
            n = g * TPG + tt
            s = (g % 2) * TPG + tt
            xr = res(("xp", s))
            for hf in range(2):
                b = yb[tt][hf]
                dve(lambda e, b=b, s=s, hf=hf: e.tensor_tensor(out=xpool[s][:, hf * 512:(hf + 1) * 512],
                                                               in0=ps[b][:, :],
                                                               in1=xpool[s][:, hf * 512:(hf + 1) * 512],
                                                               op=ALU.add),
                    (ps_r[b], xr), (xr,))
            if last_layer:
                hs = 4 + (n % 2)
                ssr = res(("ss", hs))
                act(lambda e, s=s, hs=hs: e.activation(out=junk[:], in_=xpool[s][:], func=AF.Square,
                                                       accum_out=ss[:, hs:hs + 1]),
                    (xr,), (ssr,))
                rsr = res(("rs", hs))
                act(lambda e, hs=hs: e.activation(out=rs[:, hs:hs + 1], in_=ss[:, hs:hs + 1], func=AF.Ln,
                                                  bias=cst[:, 0:1], scale=1.0 / D),
                    (ssr, consts_r), (rsr,))
                act(lambda e, hs=hs: e.activation(out=rs[:, hs:hs + 1], in_=rs[:, hs:hs + 1], func=AF.Exp,
                                                  scale=-0.5),
                    (rsr,), (rsr,))
                dve(lambda e, s=s, hs=hs: e.scalar_tensor_tensor(out=xpool[s][:], in0=xpool[s][:],
                                                                 scalar=rs[:, hs:hs + 1], in1=fnw[:],
                                                                 op0=ALU.mult, op1=ALU.mult),
                    (xr, rsr, consts_r), (xr,))
            pg.dma("pool", xo_sem[s], Xdst(l)[n * 128:(n + 1) * 128, :], xpool[s][:],
                   reads=(xr,), writes=(Xres(l + 1, n),))

    for l in range(depth):
        layer_setup(l)
        norm_a(l, 0)
        norm_b(l, 0)
        gates(l, 0)
        for g in range(NG):
            group(l, g)
            if debug and l == 0 and g == 0:
                pg.dma("sp", setup_sem, dbg, arena[:, 0:4096],
                       reads=tuple(res(("omix", c)) for c in range(16)), writes=())
            wout_stage(l, g)
    pg.finish("sp")
    pg.finish("pool")
    pg.finish("act")
    pg.emit()
    stack.close()
    return nc


def _consts():
    bf = ml_dtypes.bfloat16
    ident = np.eye(128, dtype=np.float32).astype(bf)
    p = np.arange(128)[:, None]
    c = np.arange(128)[None, :]
    mask = (p <= c).astype(np.float32).astype(bf)
    scan = np.ones((128, G), dtype=np.float32)
    scan[:, ::128] = 0.0
    scan = scan.astype(bf)
    ones = np.full((128, 128), 1.0 / 512.0, dtype=np.float32).astype(bf)
    return {"c_ident": ident, "c_mask": mask, "c_scan": scan, "c_ones": ones}


_NC_CACHE = {}


DBG_OUT = []


def run(inputs, T, depth, n_cores, debug=False):
    key = (T, depth)
    if key not in _NC_CACHE:
        _NC_CACHE[key] = build_program(T, depth, debug)
    nc = _NC_CACHE[key]
    cst = _consts()
    shared = {}
    for k in ("norm_w", "w_in", "gla_w_gate_up", "gla_b_gate", "gla_norm_w", "sc_conv_w", "cf_conv_w",
              "cf_conv_b", "cf_ln_w", "cf_ln_b", "w_out"):
        shared[k] = np.ascontiguousarray(np.asarray(inputs[k], dtype=np.float32))
    shared["final_norm_w"] = np.ascontiguousarray(np.asarray(inputs["final_norm_w"], dtype=np.float32)).reshape(1, D)
    shared.update(cst)
    x = np.asarray(inputs["x"], dtype=np.float32)
    in_maps = []
    for c in range(n_cores):
        m = dict(shared)
        m["x"] = np.ascontiguousarray(x[c])
        in_maps.append(m)
    res = run_bass_kernel_spmd(nc, in_maps, core_ids=list(range(n_cores)))
    if debug:
        DBG_OUT[:] = [np.asarray(r["dbg"]) for r in res.results]
    return np.stack([np.asarray(r["out"], dtype=np.float32) for r in res.results], axis=0)


def kernel(**inputs):
    return run(inputs, SEQ, DEPTH, NCORES)
```

```python
import math
from contextlib import ExitStack

import numpy as np
import ml_dtypes

import concourse.bass as bass
import concourse.mybir as mybir
from concourse.bass_utils import run_bass_kernel_spmd

F32 = mybir.dt.float32
BF16 = mybir.dt.bfloat16
AF = mybir.ActivationFunctionType
ALU = mybir.AluOpType

D = 1024
KC = 8
DIN = 6672
DMIX = 2048
NH = 4
DEPTH = 4
SEQ = 8192
NCORES = 8
TPG = 2
G = TPG * 128
NORM_EPS = 1e-6
LN_EPS = 1e-5
NU = 17
RING = 4

UCOL = {"q": 0, "k": 512, "v0": 1024, "v1": 1536, "gg0": 2048, "gg1": 2560,
        "sb": 3088, "sc": 3600, "sh": 4112, "sgate": 4624, "ca": 5136, "cb": 5648, "cgate": 6160}
LR_COL = 3072
UORDER = ["cb", "ca", "sc", "sh", "sgate", "sb", "cgate", "v0", "v1", "gg0", "gg1", "q", "k",
          "wo2", "wo3", "wo0", "wo1"]
UIDX = {n: i for i, n in enumerate(UORDER)}


class Res:
    __slots__ = ("w", "r")

    def __init__(self):
        self.w = None
        self.r = {}


class DSem:
    def __init__(self, sem):
        self.sem = sem
        self.cnt = 0


class Eng:
    def __init__(self, name, sem):
        self.name = name
        self.sem = sem
        self.cnt = 0
        self.seen = {}
        self.prog = []


class Prog:
    def __init__(self, nc, stack):
        self.nc = nc
        self.stack = stack
        self.E = {}
        for n in ("pe", "act", "dve", "pool", "sp"):
            self.E[n] = Eng(n, stack.enter_context(nc.semaphore("tl_" + n)))
        self.dsems = []

    def dsem(self, name):
        d = DSem(self.stack.enter_context(self.nc.semaphore(name)))
        self.dsems.append(d)
        return d

    def _wait(self, e, tok):
        sem, val, owner = tok
        k = id(sem)
        if e.seen.get(k, 0) >= val:
            return
        e.seen[k] = val
        e.prog.append(("w", sem, val))

    def _deps(self, e, reads, writes):
        for r in reads:
            if r.w is not None:
                t = r.w
                if not (t[2] is e and e.name == "pe"):
                    self._wait(e, t)
        for w in writes:
            if w.w is not None and w.w[2] is not e:
                self._wait(e, w.w)
            for t in w.r.values():
                if t[2] is not e:
                    self._wait(e, t)

    def _mark(self, tok, reads, writes):
        k = id(tok[0])
        for r in reads:
            r.r[k] = tok
        for w in writes:
            w.w = tok
            w.r = {}

    def op(self, en, fn, reads=(), writes=()):
        e = self.E[en]
        self._deps(e, reads, writes)
        e.cnt += 1
        tok = (e.sem, e.cnt, e)
        e.prog.append(("o", fn))
        self._mark(tok, reads, writes)
        return tok

    def dma(self, en, ds, out, in_, reads=(), writes=(), slow=False):
        e = self.E[en]
        self._deps(e, reads, writes)
        ds.cnt += 1
        tok = (ds.sem, 16 * ds.cnt, None)
        e.prog.append(("d", out, in_, ds.sem, slow))
        self._mark(tok, reads, writes)
        return tok

    def barrier(self):
        for e in self.E.values():
            for o in self.E.values():
                if o is not e and o.cnt > 0:
                    self._wait(e, (o.sem, o.cnt, o))
            for d in self.dsems:
                if d.cnt > 0:
                    self._wait(e, (d.sem, 16 * d.cnt, None))

    def finish(self, en):
        e = self.E[en]
        for d in self.dsems:
            if d.cnt > 0:
                self._wait(e, (d.sem, 16 * d.cnt, None))

    def emit(self):
        nc = self.nc

        def run(e, eng):
            for it in e.prog:
                if it[0] == "w":
                    eng.wait_ge(it[1], it[2])
                elif it[0] == "o":
                    it[1](eng).then_inc(e.sem, 1)
                else:
                    if it[4]:
                        eng.dma_start(out=it[1], in_=it[2], allow_slow_non_contiguous=True).then_inc(it[3], 16)
                    else:
                        eng.dma_start(out=it[1], in_=it[2]).then_inc(it[3], 16)

        with nc.Block() as blk:
            @blk.tensor
            def _(eng):
                run(self.E["pe"], eng)

            @blk.scalar
            def _(eng):
                run(self.E["act"], eng)

            @blk.vector
            def _(eng):
                run(self.E["dve"], eng)

            @blk.gpsimd
            def _(eng):
                run(self.E["pool"], eng)

            @blk.sync
            def _(eng):
                run(self.E["sp"], eng)


def build_program(T=SEQ, depth=DEPTH, debug=False):
    NT = T // 128
    NG = T // G
    nc = bass.Bass("TRN2", target_bir_lowering=False)
    stack = ExitStack()
    pg = Prog(nc, stack)

    def dram(name, shape, dt, kind):
        return nc.dram_tensor(name, list(shape), dt, kind=kind).ap()

    x_in = dram("x", [T, D], F32, "ExternalInput")
    norm_w = dram("norm_w", [depth, D], F32, "ExternalInput")
    w_in = dram("w_in", [depth, D, DIN], F32, "ExternalInput")
    w_gup = dram("gla_w_gate_up", [depth, 16, 512], F32, "ExternalInput")
    b_gate = dram("gla_b_gate", [depth, 512], F32, "ExternalInput")
    gnorm_w = dram("gla_norm_w", [depth, 256], F32, "ExternalInput")
    sc_w = dram("sc_conv_w", [depth, 3, 512], F32, "ExternalInput")
    cf_w = dram("cf_conv_w", [depth, 31, 512], F32, "ExternalInput")
    cf_b = dram("cf_conv_b", [depth, 512], F32, "ExternalInput")
    ln_w = dram("cf_ln_w", [depth, 512], F32, "ExternalInput")
    ln_b = dram("cf_ln_b", [depth, 512], F32, "ExternalInput")
    w_out = dram("w_out", [depth, DMIX, D], F32, "ExternalInput")
    fnorm_w = dram("final_norm_w", [1, D], F32, "ExternalInput")
    c_ident = dram("c_ident", [128, 128], BF16, "ExternalInput")
    c_mask = dram("c_mask", [128, 128], BF16, "ExternalInput")
    c_scan = dram("c_scan", [128, G], BF16, "ExternalInput")
    c_ones = dram("c_ones", [128, 128], BF16, "ExternalInput")
    out = dram("out", [T, D], F32, "ExternalOutput")
    dbg = dram("dbg", [128, 4096], BF16, "ExternalOutput") if debug else None
    xs = [dram("xs0", [T, D], F32, "Internal"), dram("xs1", [T, D], F32, "Internal")]
    wbf = dram("wbf", [depth * NU, 128, 4096], BF16, "Internal")

    def sb(name, shape, dt):
        return stack.enter_context(nc.sbuf_tensor(name, list(shape), dt))

    ident = sb("ident", [128, 128], BF16)
    maskt = sb("maskt", [128, 128], BF16)
    scanm = sb("scanm", [128, G], BF16)
    onesm = sb("onesm", [128, 128], BF16)
    fnw = sb("fnw", [128, D], F32)
    cst = sb("cst", [128, 4], F32)
    nwT = sb("nwT", [128, depth * 8], F32)
    gnwT = sb("gnwT", [128, depth * 2], F32)
    nbgT = sb("nbgT", [128, depth * 4], F32)
    cfbT = sb("cfbT", [128, depth * 4], F32)
    lnwT = sb("lnwT", [128, depth * 4], F32)
    lnbT = sb("lnbT", [128, depth * 4], F32)
    cfwT = sb("cfwT", [128, depth * 4, 31], F32)
    scwT = sb("scwT", [128, depth * 4, 3], F32)
    wlr = sb("wlr", [128, depth, 8, 16], BF16)
    wg = sb("wg", [16, 512], F32)
    dcf = sb("dcf", [128, 4, 31, 128], BF16)
    dsc = sb("dsc", [128, 4, 3, 128], BF16)
    ring = [sb(f"ring{i}", [128, 4096], BF16) for i in range(RING)]
    xpool = [sb(f"xp{i}", [128, D], F32) for i in range(4)]
    hbf = [sb(f"hbf{i}", [128, D], BF16) for i in range(2)]
    hT = sb("hT", [128, 8, G], BF16)
    NTMP = 6
    tmp = [sb(f"tmp{i}", [128, G], F32) for i in range(NTMP)]
    glu = [[sb(f"glu{p}{j}", [128, 30 + G], BF16) for j in range(4)] for p in range(2)]
    ubuf = [[sb(f"u{p}{j}", [128, 2 + G], BF16) for j in range(4)] for p in range(2)]
    cbf = [sb(f"cbf{j}", [128, G], BF16) for j in range(4)]
    csq = [sb(f"csq{j}", [128, G], BF16) for j in range(4)]
    mean_sb = sb("mean_sb", [128, G], F32)
    rstd_sb = sb("rstd_sb", [128, G], F32)
    t1 = [sb(f"t1{j}", [128, G], BF16) for j in range(4)]
    glr = sb("glr", [16, G], F32)
    dbuf = [sb(f"dbuf{h}", [128, G], F32) for h in range(4)]
    clast = sb("clast", [128, 4, TPG], F32)
    dec = sb("dec", [128, 4, TPG], F32)
    qi = [sb(f"qi{h}", [128, G], BF16) for h in range(4)]
    ki = [sb(f"ki{h}", [128, G], BF16) for h in range(4)]
    kiT = [sb(f"kiT{t}", [128, 512], BF16) for t in range(TPG)]
    arena = sb("arena", [128, 8192], BF16)
    omix = arena[:, 0:16 * G].rearrange("p (m c) -> p m c", c=G)
    v_sb = [arena[:, 4096 + t * 1024:4096 + (t + 1) * 1024] for t in range(TPG)]
    gsil = [arena[:, 6144 + t * 1024:6144 + (t + 1) * 1024] for t in range(TPG)]
    PTs = [sb(f"PTs{i}", [128, 4, 128], BF16) for i in range(2)]
    S = sb("S", [128, 4, 256], F32)
    Dbf = [sb(f"Dbf{i}", [128, 4, 256], BF16) for i in range(2)]
    on = [sb(f"on{i}", [128, D], BF16) for i in range(2)]
    junk = sb("junk", [128, D], BF16)
    ss = sb("ss", [128, 8], F32)
    rs = sb("rs", [128, 8], F32)
    ss4 = sb("ss4", [128, 8], F32)
    rs4 = sb("rs4", [128, 8], F32)
    stg = [sb(f"stg{i}", [128, 4096], F32) for i in range(2)]
    sbo = [arena[:, i * 4096:(i + 1) * 4096] for i in range(2)]

    ps = [stack.enter_context(nc.psum_tensor(f"ps{i}", [128, 512], F32)) for i in range(8)]
    ps_r = [Res() for _ in range(8)]
    NFB = 6
    fb = [0]
    tb = [0]

    def bank():
        b = fb[0] % NFB
        fb[0] += 1
        return b

    def tbank():
        b = NFB + (tb[0] % 2)
        tb[0] += 1
        return b

    R = {}

    def res(key):
        if key not in R:
            R[key] = Res()
        return R[key]

    setup_sem = pg.dsem("d_setup")
    stg_sem = [pg.dsem(f"d_stg{i}") for i in range(2)]
    sbo_sem = [pg.dsem(f"d_sbo{i}") for i in range(2)]
    ring_sem = [pg.dsem(f"d_ring{i}") for i in range(RING)]
    xp_sem = [pg.dsem(f"d_xp{i}") for i in range(4)]
    xo_sem = [pg.dsem(f"d_xo{i}") for i in range(4)]
    wg_sem = pg.dsem("d_wg")

    def act(fn, reads, writes):
        return pg.op("act", fn, reads, writes)

    def dve(fn, reads, writes):
        return pg.op("dve", fn, reads, writes)

    def pool(fn, reads, writes):
        return pg.op("pool", fn, reads, writes)

    def pe(fn, reads, writes):
        return pg.op("pe", fn, reads, writes)

    LNSCALE = float(-0.5 * math.log(128.0))
    consts_r = res("consts")

    def setup_dma(o, i, slow=False):
        pg.dma("sp", setup_sem, o, i, reads=(), writes=(consts_r,), slow=slow)

    setup_dma(ident[:], c_ident)
    setup_dma(maskt[:], c_mask)
    setup_dma(scanm[:], c_scan)
    setup_dma(onesm[:], c_ones)
    setup_dma(fnw[:], fnorm_w.partition_broadcast(128))
    setup_dma(nwT[:], norm_w.rearrange("l (k p) -> p (l k)", p=128), slow=True)
    setup_dma(gnwT[:], gnorm_w.rearrange("l (k p) -> p (l k)", p=128), slow=True)
    setup_dma(nbgT[:], b_gate.rearrange("l (k p) -> p (l k)", p=128), slow=True)
    setup_dma(cfbT[:], cf_b.rearrange("l (k p) -> p (l k)", p=128), slow=True)
    setup_dma(lnwT[:], ln_w.rearrange("l (k p) -> p (l k)", p=128), slow=True)
    setup_dma(lnbT[:], ln_b.rearrange("l (k p) -> p (l k)", p=128), slow=True)
    for l in range(depth):
        for j in range(4):
            setup_dma(cfwT[:, l * 4 + j, :], cf_w[l, :, j * 128:(j + 1) * 128].rearrange("k p -> p k"), slow=True)
            setup_dma(scwT[:, l * 4 + j, :], sc_w[l, :, j * 128:(j + 1) * 128].rearrange("k p -> p k"), slow=True)
    pool(lambda e: e.memset(cst[:, 0:1], float(NORM_EPS)), (), (consts_r,))
    pool(lambda e: e.memset(cst[:, 1:2], float(LN_EPS)), (), (consts_r,))
    pool(lambda e: e.memset(cst[:, 2:3], LNSCALE), (), (consts_r,))
    pool(lambda e: e.memset(cst[:, 3:4], 1.0), (), (consts_r,))
    pg.barrier()
    dve(lambda e: e.tensor_scalar(out=nbgT[:], in0=nbgT[:], scalar1=-1.0, scalar2=None, op0=ALU.mult),
        (consts_r,), (consts_r,))

    cvt_engs = ["act", "dve"]
    cvt_i = [0]

    def convert(si, views_in, views_out, scalars):
        for vin, vout, sc in zip(views_in, views_out, scalars):
            en = cvt_engs[cvt_i[0] % 2]
            cvt_i[0] += 1
            rd = (res(("stg", si)), consts_r)
            wr = (res(("sbo", si)),)
            if sc is None:
                if en == "act":
                    act(lambda e, a=vin, b=vout: e.copy(out=b, in_=a), rd, wr)
                else:
                    pg.op(en, lambda e, a=vin, b=vout: e.tensor_copy(out=b, in_=a), rd, wr)
            else:
                if en == "act":
                    act(lambda e, a=vin, b=vout, s=sc: e.activation(out=b, in_=a, func=AF.Copy, scale=s), rd, wr)
                else:
                    pg.op(en, lambda e, a=vin, b=vout, s=sc: e.tensor_scalar(
                        out=b, in0=a, scalar1=s, scalar2=None, op0=ALU.mult), rd, wr)

    cv = [0]
    for l in range(depth):
        for u, name in enumerate(UORDER):
            si = cv[0] % 2
            cv[0] += 1
            if name.startswith("wo"):
                i = int(name[2])
                src = w_out[l, i * 512:(i + 1) * 512, :].rearrange("(m p) c -> p m c", p=128)
                dst = stg[si][:].rearrange("p (m c) -> p m c", c=1024)
            else:
                c0 = UCOL[name]
                src = w_in[l, :, c0:c0 + 512].rearrange("(k p) c -> p k c", p=128)
                dst = stg[si][:].rearrange("p (k c) -> p k c", c=512)
            pg.dma("sp", stg_sem[si], dst, src, reads=(), writes=(res(("stg", si)),))
            vin, vout, scs = [], [], []
            if name.startswith("wo"):
                i = int(name[2])
                for m in range(4):
                    for hf in range(2):
                        a = m * 1024 + hf * 512
                        vin.append(stg[si][:, a:a + 512])
                        vout.append(sbo[si][:, a:a + 512])
                        scs.append(gnwT[:, l * 2 + (m % 2):l * 2 + (m % 2) + 1] if i < 2 else None)
            else:
                for k in range(8):
                    vin.append(stg[si][:, k * 512:(k + 1) * 512])
                    vout.append(sbo[si][:, k * 512:(k + 1) * 512])
                    scs.append(nwT[:, l * 8 + k:l * 8 + k + 1])
            convert(si, vin, vout, scs)
            pg.dma("pool", sbo_sem[si], wbf[l * NU + u], sbo[si], reads=(res(("sbo", si)),),
                   writes=(res(("wbf", l, u)),))
        si = cv[0] % 2
        cv[0] += 1
        src = w_in[l, :, LR_COL:LR_COL + 16].rearrange("(k p) c -> p k c", p=128)
        dst = stg[si][:, 0:128].rearrange("p (k c) -> p k c", c=16)
        pg.dma("sp", stg_sem[si], dst, src, reads=(), writes=(res(("stg", si)),), slow=True)
        for k in range(8):
            dve(lambda e, a=stg[si][:, k * 16:(k + 1) * 16], b=wlr[:, l, k, :],
                s=nwT[:, l * 8 + k:l * 8 + k + 1]: e.tensor_scalar(out=b, in0=a, scalar1=s, scalar2=None,
                                                                   op0=ALU.mult),
                (res(("stg", si)), consts_r), (res("wlr"),))
    pg.barrier()

    useq = []
    for l in range(depth):
        for g in range(NG):
            for name in UORDER:
                useq.append((l, g, name))
    loaded = [0]

    def ensure_loaded(upto):
        while loaded[0] <= upto and loaded[0] < len(useq):
            i = loaded[0]
            l, g, name = useq[i]
            s = i % RING
            pg.dma("sp", ring_sem[s], ring[s][:], wbf[l * NU + UIDX[name]],
                   reads=(res(("wbf", l, UIDX[name])),), writes=(res(("ring", s)),))
            loaded[0] += 1

    upos = [0]

    def next_unit(l, g, name, hold=0):
        i = upos[0]
        assert useq[i] == (l, g, name), (useq[i], l, g, name)
        upos[0] += 1
        ensure_loaded(i + RING - 1 - hold)
        s = i % RING
        return ring[s], res(("ring", s))

    tmp_i = [0]

    def gettmp():
        i = tmp_i[0] % NTMP
        tmp_i[0] += 1
        return tmp[i], res(("tmp", i))

    hT_r = [res(("hT", t)) for t in range(TPG)]

    def Xsrc(l):
        return x_in if l == 0 else xs[(l - 1) % 2]

    def Xdst(l):
        return out if l == depth - 1 else xs[l % 2]

    def Xres(l, n):
        return res(("X", l, n))

    def layer_setup(l):
        pg.dma("sp", wg_sem, wg[:], w_gup[l], reads=(), writes=(res("wg"),))
        for j in range(4):
            for k in range(31):
                en = "dve"
                pg.op(en, lambda e, o=dcf[:, j, k, :], s=cfwT[:, l * 4 + j, k:k + 1]: e.tensor_scalar(
                    out=o, in0=ident[:], scalar1=s, scalar2=None, op0=ALU.mult),
                    (consts_r,), (res("dcf"),))
            for k in range(3):
                dve(lambda e, o=dsc[:, j, k, :], s=scwT[:, l * 4 + j, k:k + 1]: e.tensor_scalar(
                    out=o, in0=ident[:], scalar1=s, scalar2=None, op0=ALU.mult),
                    (consts_r,), (res("dsc"),))
        pool(lambda e: e.memset(S[:], 0.0), (), (res("S"),))

    xslot = [0]

    def norm_a(l, g):
        for tt in range(TPG):
            n = g * TPG + tt
            s = (g % 2) * TPG + tt
            hs = n % 2
            xr = res(("xp", s))
            pg.dma("sp", xp_sem[s], xpool[s][:], Xsrc(l)[n * 128:(n + 1) * 128, :],
                   reads=(Xres(l, n),), writes=(xr,))
            ssr = res(("ss", hs))
            act(lambda e, s=s, hs=hs: e.activation(out=junk[:], in_=xpool[s][:], func=AF.Square,
                                                   accum_out=ss[:, hs:hs + 1]),
                (xr,), (ssr,))
            rsr = res(("rs", hs))
            act(lambda e, hs=hs: e.activation(out=rs[:, hs:hs + 1], in_=ss[:, hs:hs + 1], func=AF.Ln,
                                              bias=cst[:, 0:1], scale=1.0 / D),
                (ssr, consts_r), (rsr,))
            act(lambda e, hs=hs: e.activation(out=rs[:, hs:hs + 1], in_=rs[:, hs:hs + 1], func=AF.Exp,
                                              scale=-0.5),
                (rsr,), (rsr,))
            hr = res(("hbf", hs))
            act(lambda e, s=s, hs=hs: e.activation(out=hbf[hs][:], in_=xpool[s][:], func=AF.Copy,
                                                   scale=rs[:, hs:hs + 1]),
                (xr, rsr), (hr,))

    def norm_b(l, g):
        for tt in range(TPG):
            n = g * TPG + tt
            hs = n % 2
            hr = res(("hbf", hs))
            b = tbank()
            pv = ps[b].bitcast(BF16)

            def tr(e, hs=hs, pv=pv):
                ins = None
                for k in range(8):
                    ins = e.transpose(out=pv[:, k * 128:(k + 1) * 128], in_=hbf[hs][:, k * 128:(k + 1) * 128],
                                      identity=ident[:])
                return ins
            pe(tr, (hr, consts_r), (ps_r[b],))
            dve(lambda e, pv=pv, tt=tt: e.tensor_copy(out=hT[:, :, tt * 128:(tt + 1) * 128],
                                                      in_=pv[:, 0:1024].rearrange("p (k c) -> p k c", c=128)),
                (ps_r[b],), (hT_r[tt],))

    def fm_chunk(W, Wr, j, M=128):
        b = bank()

        def mm(e, b=b, j=j):
            ins = None
            for k in range(8):
                ins = e.matmul(ps[b][0:M, 0:G], lhsT=W[:, k * 512 + j * 128:k * 512 + j * 128 + M],
                               rhs=hT[:, k, :], start=(k == 0), stop=(k == 7))
            return ins
        pe(mm, (Wr,) + tuple(hT_r), (ps_r[b],))
        return b

    def tm_tile(W, Wr, tt):
        b = bank()

        def mm(e, b=b, tt=tt):
            ins = None
            for k in range(8):
                ins = e.matmul(ps[b][:, :], lhsT=hT[:, k, tt * 128:(tt + 1) * 128],
                               rhs=W[:, k * 512:(k + 1) * 512], start=(k == 0), stop=(k == 7))
            return ins
        pe(mm, (Wr, hT_r[tt]), (ps_r[b],))
        return b


    def gates(l, g):
        b = bank()

        def mm_lr(e, b=b):
            ins = None
            for k in range(8):
                ins = e.matmul(ps[b][0:16, 0:G], lhsT=wlr[:, l, k, :], rhs=hT[:, k, :],
                               start=(k == 0), stop=(k == 7))
            return ins
        pe(mm_lr, (res("wlr"),) + tuple(hT_r), (ps_r[b],))
        glr_r = res("glr")
        act(lambda e, b=b: e.copy(out=glr[:], in_=ps[b][0:16, 0:G]), (ps_r[b],), (glr_r,))
        for h in range(4):
            b = bank()
            pe(lambda e, b=b, h=h: e.matmul(ps[b][:, 0:G], lhsT=wg[:, h * 128:(h + 1) * 128], rhs=glr[:],
                                            start=True, stop=True),
               (res("wg"), glr_r), (ps_r[b],))
            t_e, t_er = gettmp()
            act(lambda e, b=b, h=h, t=t_e: e.activation(out=t[:], in_=ps[b][:, 0:G], func=AF.Exp,
                                                        bias=nbgT[:, l * 4 + h:l * 4 + h + 1], scale=-1.0),
                (ps_r[b], consts_r), (t_er,))
            act(lambda e, t=t_e: e.activation(out=t[:], in_=t[:], func=AF.Ln, bias=cst[:, 3:4], scale=1.0),
                (t_er,), (t_er,))
            dr = res(("d", h))
            dve(lambda e, t=t_e, h=h: e.tensor_tensor_scan(out=dbuf[h][:], data0=scanm[:], data1=t[:],
                                                           initial=0.0, op0=ALU.mult, op1=ALU.add),
                (t_er, consts_r), (dr,))
            clr = res(("cl", h))
            dve(lambda e, h=h: e.tensor_copy(
                out=clast[:, h, :], in_=dbuf[h][:].rearrange("p (t c) -> p t c", c=128)[:, :, 127]),
                (dr,), (clr,))
            dve(lambda e, h=h: e.tensor_tensor(
                out=dbuf[h][:].rearrange("p (t c) -> p t c", c=128),
                in0=dbuf[h][:].rearrange("p (t c) -> p t c", c=128),
                in1=clast[:, h, :].unsqueeze(2).to_broadcast([128, TPG, 128]), op=ALU.subtract),
                (dr, clr), (dr,))
            act(lambda e, h=h: e.activation(out=dec[:, h, :], in_=clast[:, h, :], func=AF.Exp,
                                            scale=-1.0 / 16.0),
                (clr,), (res(("dec", h)),))


    def group(l, g):
        par = g % 2
        last_layer = (l == depth - 1)
        Wb, Wbr = next_unit(l, g, "cb")
        Wa, War = next_unit(l, g, "ca", hold=1)
        for j in range(4):
            gr = res(("glu", par, j))
            if g == 0:
                pool(lambda e, j=j: e.memset(glu[par][j][:, 0:30], 0.0), (), (gr,))
            else:
                pool(lambda e, j=j: e.tensor_copy(out=glu[par][j][:, 0:30], in_=glu[1 - par][j][:, G:G + 30]),
                     (res(("glu", 1 - par, j)),), (gr,))
            bb = fm_chunk(Wb, Wbr, j)
            t_s, t_sr = gettmp()
            act(lambda e, b=bb, t=t_s: e.activation(out=t[:], in_=ps[b][:, 0:G], func=AF.Sigmoid),
                (ps_r[bb],), (t_sr,))
            ba = fm_chunk(Wa, War, j)
            dve(lambda e, b=ba, t=t_s, j=j: e.tensor_tensor(out=glu[par][j][:, 30:30 + G], in0=ps[b][:, 0:G],
                                                            in1=t[:], op=ALU.mult),
                (ps_r[ba], t_sr), (gr,))
        Wc, Wcr = next_unit(l, g, "sc")
        Wh, Whr = next_unit(l, g, "sh", hold=1)
        for j in range(4):
            ur = res(("u", par, j))
            if g == 0:
                pool(lambda e, j=j: e.memset(ubuf[par][j][:, 0:2], 0.0), (), (ur,))
            else:
                pool(lambda e, j=j: e.tensor_copy(out=ubuf[par][j][:, 0:2], in_=ubuf[1 - par][j][:, G:G + 2]),
                     (res(("u", 1 - par, j)),), (ur,))
            bc = fm_chunk(Wc, Wcr, j)
            t_s, t_sr = gettmp()
            dve(lambda e, b=bc, t=t_s: e.tensor_copy(out=t[:], in_=ps[b][:, 0:G]), (ps_r[bc],), (t_sr,))
            bh = fm_chunk(Wh, Whr, j)
            dve(lambda e, b=bh, t=t_s, j=j: e.tensor_tensor(out=ubuf[par][j][:, 2:2 + G], in0=ps[b][:, 0:G],
                                                            in1=t[:], op=ALU.mult),
                (ps_r[bh], t_sr), (ur,))
        for j in range(4):
            b = bank()

            def mmc(e, b=b, j=j):
                ins = None
                for k in range(31):
                    ins = e.matmul(ps[b][:, 0:G], lhsT=dcf[:, j, k, :], rhs=glu[par][j][:, k:k + G],
                                   start=(k == 0), stop=(k == 30))
                return ins
            pe(mmc, (res("dcf"), res(("glu", par, j))), (ps_r[b],))
            act(lambda e, b=b, j=j: e.activation(out=cbf[j][:], in_=ps[b][:, 0:G], func=AF.Identity,
                                                 bias=cfbT[:, l * 4 + j:l * 4 + j + 1], scale=1.0),
                (ps_r[b], consts_r), (res(("cbf", j)),))
            act(lambda e, b=b, j=j: e.activation(out=csq[j][:], in_=ps[b][:, 0:G], func=AF.Square,
                                                 bias=cfbT[:, l * 4 + j:l * 4 + j + 1], scale=1.0),
                (ps_r[b], consts_r), (res(("csq", j)),))
        Wsg, Wsgr = next_unit(l, g, "sgate")
        Wsb, Wsbr = next_unit(l, g, "sb", hold=1)
        for j in range(4):
            bg_ = fm_chunk(Wsg, Wsgr, j)
            t_s, t_sr = gettmp()
            act(lambda e, b=bg_, t=t_s: e.activation(out=t[:], in_=ps[b][:, 0:G], func=AF.Silu),
                (ps_r[bg_],), (t_sr,))
            bs = fm_chunk(Wsb, Wsbr, j)
            dve(lambda e, b=bs, t=t_s, j=j: e.tensor_tensor(out=t1[j][:], in0=ps[b][:, 0:G], in1=t[:],
                                                            op=ALU.mult),
                (ps_r[bs], t_sr), (res(("t1", j)),))
        for j in range(4):
            b = bank()

            def mms(e, b=b, j=j):
                ins = None
                for k in range(3):
                    ins = e.matmul(ps[b][:, 0:G], lhsT=dsc[:, j, k, :], rhs=ubuf[par][j][:, k:k + G],
                                   start=(k == 0), stop=(k == 2))
                return ins
            pe(mms, (res("dsc"), res(("u", par, j))), (ps_r[b],))
            dve(lambda e, b=b, j=j: e.tensor_tensor(out=omix[:, 8 + j, :], in0=ps[b][:, 0:G], in1=t1[j][:],
                                                    op=ALU.mult),
                (ps_r[b], res(("t1", j))), (res(("omix", 8 + j)),))
        b1 = bank()

        def mm_mean(e, b=b1):
            ins = None
            for j in range(4):
                ins = e.matmul(ps[b][:, 0:G], lhsT=onesm[:], rhs=cbf[j][:], start=(j == 0), stop=(j == 3))
            return ins
        pe(mm_mean, tuple(res(("cbf", j)) for j in range(4)) + (consts_r,), (ps_r[b1],))
        b2 = bank()

        def mm_msq(e, b=b2):
            ins = None
            for j in range(4):
                ins = e.matmul(ps[b][:, 0:G], lhsT=onesm[:], rhs=csq[j][:], start=(j == 0), stop=(j == 3))
            return ins
        pe(mm_msq, tuple(res(("csq", j)) for j in range(4)) + (consts_r,), (ps_r[b2],))
        mr = res("mean")
        rr = res("rstd")
        act(lambda e, b=b1: e.copy(out=mean_sb[:], in_=ps[b][:, 0:G]), (ps_r[b1],), (mr,))
        t_m, t_mr = gettmp()
        dve(lambda e, t=t_m: e.tensor_tensor(out=t[:], in0=mean_sb[:], in1=mean_sb[:], op=ALU.mult),
            (mr,), (t_mr,))
        dve(lambda e, t=t_m, b=b2: e.tensor_tensor(out=t[:], in0=ps[b][:, 0:G], in1=t[:], op=ALU.subtract),
            (ps_r[b2], t_mr), (t_mr,))
        dve(lambda e, t=t_m: e.tensor_scalar(out=t[:], in0=t[:], scalar1=0.0, scalar2=None, op0=ALU.max),
            (t_mr,), (t_mr,))
        act(lambda e, t=t_m: e.activation(out=t[:], in_=t[:], func=AF.Ln, bias=cst[:, 1:2], scale=1.0),
            (t_mr, consts_r), (t_mr,))
        act(lambda e, t=t_m: e.activation(out=rstd_sb[:], in_=t[:], func=AF.Exp, scale=-0.5),
            (t_mr,), (rr,))
        for j in range(4):
            t_y, t_yr = gettmp()
            pool(lambda e, t=t_y, j=j: e.tensor_tensor(out=t[:], in0=cbf[j][:], in1=mean_sb[:],
                                                       op=ALU.subtract),
                 (res(("cbf", j)), mr), (t_yr,))
            pool(lambda e, t=t_y: e.tensor_tensor(out=t[:], in0=t[:], in1=rstd_sb[:], op=ALU.mult),
                 (t_yr, rr), (t_yr,))
            act(lambda e, t=t_y, j=j: e.activation(out=omix[:, 12 + j, :], in_=t[:], func=AF.Silu,
                                                   bias=lnbT[:, l * 4 + j:l * 4 + j + 1],
                                                   scale=lnwT[:, l * 4 + j:l * 4 + j + 1]),
                (t_yr, consts_r), (res(("omix", 12 + j)),))
        W, Wr = next_unit(l, g, "cgate")
        for j in range(4):
            b = fm_chunk(W, Wr, j)
            t_s, t_sr = gettmp()
            act(lambda e, b=b, t=t_s: e.activation(out=t[:], in_=ps[b][:, 0:G], func=AF.Silu),
                (ps_r[b],), (t_sr,))
            pool(lambda e, t=t_s, j=j: e.tensor_tensor(out=omix[:, 12 + j, :], in0=omix[:, 12 + j, :], in1=t[:],
                                                       op=ALU.mult),
                 (t_sr, res(("omix", 12 + j))), (res(("omix", 12 + j)),))
        if g + 1 < NG:
            norm_a(l, g + 1)
        for hf, name in enumerate(("v0", "v1")):
            W, Wr = next_unit(l, g, name)
            for tt in range(TPG):
                b = tm_tile(W, Wr, tt)
                dve(lambda e, b=b, tt=tt, hf=hf: e.tensor_copy(out=v_sb[tt][:, hf * 512:(hf + 1) * 512],
                                                               in_=ps[b][:, :]),
                    (ps_r[b],), (res(("v", tt)),))
        for hf, name in enumerate(("gg0", "gg1")):
            W, Wr = next_unit(l, g, name)
            for tt in range(TPG):
                b = tm_tile(W, Wr, tt)
                act(lambda e, b=b, tt=tt, hf=hf: e.activation(out=gsil[tt][:, hf * 512:(hf + 1) * 512],
                                                              in_=ps[b][:, :], func=AF.Silu),
                    (ps_r[b],), (res(("gsil", tt)),))
        for name, dst, sc_, bi_ in (("q", qi, -1.0 / 16.0, cst[:, 2:3]), ("k", ki, 1.0 / 16.0, 0.0)):
            W, Wr = next_unit(l, g, name)
            facs = []
            for h in range(4):
                t_f, t_fr = gettmp()
                act(lambda e, t=t_f, h=h, sc_=sc_, bi_=bi_: e.activation(out=t[:], in_=dbuf[h][:], func=AF.Exp,
                                                                         bias=bi_, scale=sc_),
                    (res(("d", h)),), (t_fr,))
                facs.append((t_f, t_fr))
            for h in range(4):
                b = fm_chunk(W, Wr, h)
                t_f, t_fr = facs[h]
                dve(lambda e, b=b, t=t_f, h=h, dst=dst: e.tensor_tensor(out=dst[h][:], in0=ps[b][:, 0:G],
                                                                        in1=t[:], op=ALU.mult),
                    (ps_r[b], t_fr), (res((name + "i", h)),))
        if g + 1 < NG:
            norm_b(l, g + 1)
        def tinfo(tt):
            n = g * TPG + tt
            return n % 2, slice(tt * 128, (tt + 1) * 128)

        def e_kiT(tt):
            b = tbank()
            pv = ps[b].bitcast(BF16)
            x2, tsl = tinfo(tt)

            def trk(e, pv=pv, tsl=tsl):
                ins = None
                for h in range(4):
                    ins = e.transpose(out=pv[:, h * 128:(h + 1) * 128], in_=ki[h][:, tsl], identity=ident[:])
                return ins
            pe(trk, tuple(res(("ki", h)) for h in range(4)) + (consts_r,), (ps_r[b],))
            dve(lambda e, pv=pv, tt=tt: e.tensor_copy(out=kiT[tt][:], in_=pv[:, 0:512]), (ps_r[b],),
                (res(("kiT", tt)),))

        def e_PT(tt):
            x2, tsl = tinfo(tt)
            b = bank()

            def mmp(e, b=b, tsl=tsl):
                ins = None
                for h in range(4):
                    ins = e.matmul(ps[b][:, h * 128:(h + 1) * 128], lhsT=ki[h][:, tsl], rhs=qi[h][:, tsl],
                                   start=True, stop=True)
                return ins
            pe(mmp, tuple(res(("ki", h)) for h in range(4)) + tuple(res(("qi", h)) for h in range(4)),
               (ps_r[b],))
            dve(lambda e, b=b, x2=x2: e.tensor_tensor(
                out=PTs[x2][:], in0=ps[b][:, :].rearrange("p (h c) -> p h c", c=128),
                in1=maskt[:].unsqueeze(1).to_broadcast([128, 4, 128]), op=ALU.mult),
                (ps_r[b], consts_r), (res(("PTs", x2)),))

        def e_state(tt):
            x2, tsl = tinfo(tt)
            dbr = res(("Dbf", x2))
            for h in range(4):
                act(lambda e, h=h, x2=x2, tt=tt: e.activation(out=Dbf[x2][:, h, :], in_=S[:, h, :], func=AF.Copy,
                                                              scale=dec[:, h, tt:tt + 1]),
                    (res("S"), res(("dec", h))), (dbr,))
            bkv = [bank(), bank()]
            for hp in range(2):
                def mmkv(e, b=bkv[hp], hp=hp, tt=tt):
                    ins = None
                    for hh in range(2):
                        h = hp * 2 + hh
                        ins = e.matmul(ps[b][:, hh * 256:(hh + 1) * 256], lhsT=kiT[tt][:, h * 128:(h + 1) * 128],
                                       rhs=v_sb[tt][:, h * 256:(h + 1) * 256], start=True, stop=True)
                    return ins
                pe(mmkv, (res(("kiT", tt)), res(("v", tt))), (ps_r[bkv[hp]],))
            for h in range(4):
                b = bkv[h // 2]
                dve(lambda e, b=b, h=h, tt=tt: e.scalar_tensor_tensor(
                    out=S[:, h, :], in0=S[:, h, :], scalar=dec[:, h, tt:tt + 1],
                    in1=ps[b][:, (h % 2) * 256:(h % 2 + 1) * 256], op0=ALU.mult, op1=ALU.add),
                    (res("S"), res(("dec", h)), ps_r[b]), (res("S"),))

        def e_o(tt):
            x2, tsl = tinfo(tt)
            bo = [bank(), bank()]
            for hp in range(2):
                def mmo(e, b=bo[hp], hp=hp, tt=tt, x2=x2, tsl=tsl):
                    ins = None
                    for hh in range(2):
                        h = hp * 2 + hh
                        e.matmul(ps[b][:, hh * 256:(hh + 1) * 256], lhsT=PTs[x2][:, h, :],
                                 rhs=v_sb[tt][:, h * 256:(h + 1) * 256], start=True, stop=False)
                        ins = e.matmul(ps[b][:, hh * 256:(hh + 1) * 256], lhsT=qi[h][:, tsl],
                                       rhs=Dbf[x2][:, h, :], start=False, stop=True)
                    return ins
                pe(mmo, (res(("PTs", x2)), res(("v", tt)), res(("Dbf", x2))) +
                   tuple(res(("qi", h)) for h in range(4)), (ps_r[bo[hp]],))
            s4r = res(("ss4", x2))
            for h in range(4):
                b = bo[h // 2]
                act(lambda e, b=b, h=h, x2=x2: e.activation(
                    out=junk[:, 0:256], in_=ps[b][:, (h % 2) * 256:(h % 2 + 1) * 256], func=AF.Square,
                    accum_out=ss4[:, x2 * 4 + h:x2 * 4 + h + 1]),
                    (ps_r[b],), (s4r,))
            r4r = res(("rs4", x2))
            act(lambda e, x2=x2: e.activation(out=rs4[:, x2 * 4:x2 * 4 + 4], in_=ss4[:, x2 * 4:x2 * 4 + 4],
                                              func=AF.Ln, bias=cst[:, 0:1], scale=1.0 / 256.0),
                (s4r, consts_r), (r4r,))
            act(lambda e, x2=x2: e.activation(out=rs4[:, x2 * 4:x2 * 4 + 4], in_=rs4[:, x2 * 4:x2 * 4 + 4],
                                              func=AF.Exp, scale=-0.5),
                (r4r,), (r4r,))
            onr = res(("on", x2))
            for h in range(4):
                b = bo[h // 2]
                dve(lambda e, b=b, h=h, x2=x2, tt=tt: e.scalar_tensor_tensor(
                    out=on[x2][:, h * 256:(h + 1) * 256], in0=ps[b][:, (h % 2) * 256:(h % 2 + 1) * 256],
                    scalar=rs4[:, x2 * 4 + h:x2 * 4 + h + 1], in1=gsil[tt][:, h * 256:(h + 1) * 256],
                    op0=ALU.mult, op1=ALU.mult),
                    (ps_r[b], r4r, res(("gsil", tt))), (onr,))

        def e_oaT(tt):
            x2, tsl = tinfo(tt)
            b = tbank()
            pv = ps[b].bitcast(BF16)

            def tro(e, pv=pv, x2=x2):
                ins = None
                for c in range(8):
                    ins = e.transpose(out=pv[:, c * 128:(c + 1) * 128], in_=on[x2][:, c * 128:(c + 1) * 128],
                                      identity=ident[:])
                return ins
            pe(tro, (res(("on", x2)), consts_r), (ps_r[b],))
            dve(lambda e, pv=pv, tsl=tsl: e.tensor_copy(out=omix[:, 0:8, tsl],
                                                        in_=pv[:, 0:1024].rearrange("p (k c) -> p k c", c=128)),
                (ps_r[b],), tuple(res(("omix", c)) for c in range(8)))

        for tt in range(TPG):
            e_kiT(tt)
        for tt in range(TPG):
            e_PT(tt)
        for tt in range(TPG):
            e_state(tt)
        for tt in range(TPG):
            e_o(tt)
        if g + 1 < NG:
            gates(l, g + 1)
        for tt in range(TPG):
            e_oaT(tt)
        return

    def wout_stage(l, g):
        last_layer = (l == depth - 1)
        yb = [[bank(), bank()] for _ in range(TPG)]
        for io, i in enumerate((2, 3, 0, 1)):
            W, Wr = next_unit(l, g, f"wo{i}")
            for tt in range(TPG):
                for hf in range(2):
                    b = yb[tt][hf]

                    def mmy(e, b=b, i=i, io=io, tt=tt, hf=hf, W=W):
                        ins = None
                        for m in range(4):
                            mc = i * 4 + m
                            ins = e.matmul(ps[b][:, :], lhsT=omix[:, mc, tt * 128:(tt + 1) * 128],
                                           rhs=W[:, m * 1024 + hf * 512:m * 1024 + (hf + 1) * 512],
                                           start=(io == 0 and m == 0), stop=(io == 3 and m == 3))
                        return ins
                    pe(mmy, (Wr,) + tuple(res(("omix", i * 4 + m)) for m in range(4)), (ps_r[b],))
        for tt in range(TPG):
            n = g * TPG + tt
            s = (g % 2) * TPG + tt
            xr = res(("xp", s))
            for hf in range(2):
                b = yb[tt][hf]
                dve(lambda e, b=b, s=s, hf=hf: e.tensor_tensor(out=xpool[s][:, hf * 512:(hf + 1) * 512],
                                                               in0=ps[b][:, :],
                                                               in1=xpool[s][:, hf * 512:(hf + 1) * 512],
                                                               op=ALU.add),
                    (ps_r[b], xr), (xr,))
            if last_layer:
                hs = 4 + (n % 2)
                ssr = res(("ss", hs))
                act(lambda e, s=s, hs=hs: e.activation(out=junk[:], in_=xpool[s][:], func=AF.Square,
                                                       accum_out=ss[:, hs:hs + 1]),
                    (xr,), (ssr,))
                rsr = res(("rs", hs))
                act(lambda e, hs=hs: e.activation(out=rs[:, hs:hs + 1], in_=ss[:, hs:hs + 1], func=AF.Ln,
                                                  bias=cst[:, 0:1], scale=1.0 / D),
                    (ssr, consts_r), (rsr,))
                act(lambda e, hs=hs: e.activation(out=rs[:, hs:hs + 1], in_=rs[:, hs:hs + 1], func=AF.Exp,
                                                  scale=-0.5),
                    (rsr,), (rsr,))
                dve(lambda e, s=s, hs=hs: e.scalar_tensor_tensor(out=xpool[s][:], in0=xpool[s][:],
                                                                 scalar=rs[:, hs:hs + 1], in1=fnw[:],
                                                                 op0=ALU.mult, op1=ALU.mult),
                    (xr, rsr, consts_r), (xr,))
            pg.dma("pool", xo_sem[s], Xdst(l)[n * 128:(n + 1) * 128, :], xpool[s][:],
                   reads=(xr,), writes=(Xres(l + 1, n),))

    for l in range(depth):
        layer_setup(l)
        norm_a(l, 0)
        norm_b(l, 0)
        gates(l, 0)
        for g in range(NG):
            group(l, g)
            if debug and l == 0 and g == 0:
                pg.dma("sp", setup_sem, dbg, arena[:, 0:4096],
                       reads=tuple(res(("omix", c)) for c in range(16)), writes=())
            wout_stage(l, g)
    pg.finish("sp")
    pg.finish("pool")
    pg.finish("act")
    pg.emit()
    stack.close()
    return nc


def _consts():
    bf = ml_dtypes.bfloat16
    ident = np.eye(128, dtype=np.float32).astype(bf)
    p = np.arange(128)[:, None]
    c = np.arange(128)[None, :]
    mask = (p <= c).astype(np.float32).astype(bf)
    scan = np.ones((128, G), dtype=np.float32)
    scan[:, ::128] = 0.0
    scan = scan.astype(bf)
    ones = np.full((128, 128), 1.0 / 512.0, dtype=np.float32).astype(bf)
    return {"c_ident": ident, "c_mask": mask, "c_scan": scan, "c_ones": ones}


_NC_CACHE = {}


DBG_OUT = []


def run(inputs, T, depth, n_cores, debug=False):
    key = (T, depth)
    if key not in _NC_CACHE:
        _NC_CACHE[key] = build_program(T, depth, debug)
    nc = _NC_CACHE[key]
    cst = _consts()
    shared = {}
    for k in ("norm_w", "w_in", "gla_w_gate_up", "gla_b_gate", "gla_norm_w", "sc_conv_w", "cf_conv_w",
              "cf_conv_b", "cf_ln_w", "cf_ln_b", "w_out"):
        shared[k] = np.ascontiguousarray(np.asarray(inputs[k], dtype=np.float32))
    shared["final_norm_w"] = np.ascontiguousarray(np.asarray(inputs["final_norm_w"], dtype=np.float32)).reshape(1, D)
    shared.update(cst)
    x = np.asarray(inputs["x"], dtype=np.float32)
    in_maps = []
    for c in range(n_cores):
        m = dict(shared)
        m["x"] = np.ascontiguousarray(x[c])
        in_maps.append(m)
    res = run_bass_kernel_spmd(nc, in_maps, core_ids=list(range(n_cores)))
    if debug:
        DBG_OUT[:] = [np.asarray(r["dbg"]) for r in res.results]
    return np.stack([np.asarray(r["out"], dtype=np.float32) for r in res.results], axis=0)


def kernel(**inputs):
    return run(inputs, SEQ, DEPTH, NCORES)
```
